# Optimizing a Trainium2 kernel written in Bass

```python
import jax, jax.numpy as jnp
from jax import lax
import numpy as np

D_MODEL = 1024
BATCH = 4
SEQ = 8192
DEPTH = 4

CHUNK = 64
N_MIXERS = 2
CONV_WIDTH = 31
RET_HEADS = 4
RET_QK_DIM = D_MODEL // RET_HEADS
RET_V_DIM = 2 * D_MODEL // RET_HEADS
RET_QK_TOTAL = RET_HEADS * RET_QK_DIM
RET_V_TOTAL = RET_HEADS * RET_V_DIM
RET_IN_WIDTH = 2 * RET_QK_TOTAL + 2 * RET_V_TOTAL
D_FF = 4 * D_MODEL
ROPE_BASE = 10000.0
EPS = 1e-6
N_CONV_LAYERS = (DEPTH + 1) // 2
N_RET_LAYERS = DEPTH // 2

kernel_name = "hybrid_conformer_retention_adaln_trunk"


def rmsnorm(x, g):
    xf = x.astype(jnp.float32)
    y = xf * lax.rsqrt(jnp.mean(xf * xf, axis=-1, keepdims=True) + EPS)
    return (y * g.astype(jnp.float32)).astype(x.dtype)


def modulate(h, shift, scale):
    return h * (1.0 + scale[:, None, :]) + shift[:, None, :]


def conformer_conv(h, w_pw1, b_pw1, w_dw, b_dw, ln_g, ln_b, w_pw2, b_pw2):
    u = h @ w_pw1 + b_pw1
    a, g = jnp.split(u, 2, axis=-1)
    u = a * jax.nn.sigmoid(g)
    u = lax.conv_general_dilated(
        u, w_dw[:, None, :], window_strides=(1,), padding=[(CONV_WIDTH - 1, 0)],
        dimension_numbers=('NWC', 'WIO', 'NWC'), feature_group_count=D_MODEL) + b_dw
    uf = u.astype(jnp.float32)
    mu = jnp.mean(uf, axis=-1, keepdims=True)
    var = jnp.mean(jnp.square(uf - mu), axis=-1, keepdims=True)
    u = ((uf - mu) * lax.rsqrt(var + EPS) * ln_g + ln_b).astype(h.dtype)
    u = jax.nn.silu(u)
    return u @ w_pw2 + b_pw2


def rope_tables(seq):
    pos = jnp.arange(seq, dtype=jnp.float32)
    inv = ROPE_BASE ** (-jnp.arange(0, RET_QK_DIM, 2, dtype=jnp.float32) / RET_QK_DIM)
    ang = pos[:, None] * inv[None, :]
    return jnp.cos(ang), jnp.sin(ang)


def apply_rope(x, cos, sin):
    half = RET_QK_DIM // 2
    x1, x2 = x[..., :half], x[..., half:]
    c = cos[None, :, None, :].astype(x.dtype)
    s = sin[None, :, None, :].astype(x.dtype)
    return jnp.concatenate([x1 * c - x2 * s, x2 * c + x1 * s], axis=-1)


def retention(h, w_in, gn_g, gn_b, w_out, cos, sin, log_gamma):
    b, s, _ = h.shape
    nc = s // CHUNK
    proj = h @ w_in
    q, k, v, gate = jnp.split(
        proj, [RET_QK_TOTAL, 2 * RET_QK_TOTAL, 2 * RET_QK_TOTAL + RET_V_TOTAL], axis=-1)
    q = apply_rope(q.reshape(b, s, RET_HEADS, RET_QK_DIM), cos, sin)
    k = apply_rope(k.reshape(b, s, RET_HEADS, RET_QK_DIM), cos, sin) * (RET_QK_DIM ** -0.5)
    v = v.reshape(b, s, RET_HEADS, RET_V_DIM)

    def to_chunks(t):
        return t.reshape(b, nc, CHUNK, RET_HEADS, t.shape[-1]).transpose(0, 1, 3, 2, 4)

    qc, kc, vc = to_chunks(q), to_chunks(k), to_chunks(v)
    idx = jnp.arange(CHUNK, dtype=jnp.float32)
    d_intra = jnp.exp(log_gamma[:, None, None] * jnp.abs(idx[:, None] - idx[None, :]))
    scores = jnp.einsum('bnhcd,bnhed->bnhce', qc, kc) * d_intra.astype(qc.dtype)
    intra = jnp.einsum('bnhce,bnhef->bnhcf', scores, vc)

    xi = jnp.exp(log_gamma[:, None] * (idx + 1.0))
    zeta = jnp.exp(log_gamma[:, None] * (CHUNK - 1.0 - idx))
    chunk_decay = jnp.exp(log_gamma * CHUNK)

    def step(state, inp):
        qj, kj, vj = inp
        cross = jnp.einsum('bhcd,bhdf->bhcf', qj * xi[..., None], state)
        state = state * chunk_decay[:, None, None] + jnp.einsum(
            'bhcd,bhcf->bhdf', kj * zeta[..., None], vj)
        return state, cross

    state0 = jnp.zeros((b, RET_HEADS, RET_QK_DIM, RET_V_DIM), jnp.float32)
    xs = (qc.transpose(1, 0, 2, 3, 4), kc.transpose(1, 0, 2, 3, 4), vc.transpose(1, 0, 2, 3, 4))
    _, cross = lax.scan(step, state0, xs)
    y = intra + cross.transpose(1, 0, 2, 3, 4).astype(intra.dtype)
    y = y.transpose(0, 1, 3, 2, 4).reshape(b, s, RET_HEADS, RET_V_DIM)
    yf = y.astype(jnp.float32)
    mu = jnp.mean(yf, axis=-1, keepdims=True)
    var = jnp.mean(jnp.square(yf - mu), axis=-1, keepdims=True)
    y = ((yf - mu) * lax.rsqrt(var + EPS) * gn_g + gn_b).astype(h.dtype)
    y = jax.nn.silu(gate) * y.reshape(b, s, RET_V_TOTAL)
    return y @ w_out


def setup_inputs(seed: int = 0) -> dict:
    key = jax.random.key(seed)
    ks = jax.random.split(key, 24)
    f32 = jnp.float32
    D = D_MODEL

    def nrm(k, shape, std):
        return jax.random.normal(k, shape, f32) * std

    return {
        "x": nrm(ks[0], (BATCH, SEQ, D), 1.0),
        "c": nrm(ks[1], (BATCH, D), 1.0),
        "ada_w": nrm(ks[2], (DEPTH, D, 6 * D), 0.5 * D ** -0.5),
        "ada_b": nrm(ks[3], (DEPTH, 6 * D), 0.02),
        "norm_mix_g": 1.0 + nrm(ks[4], (DEPTH, D), 0.02),
        "norm_mlp_g": 1.0 + nrm(ks[5], (DEPTH, D), 0.02),
        "conv_w_pw1": nrm(ks[6], (N_CONV_LAYERS, D, 2 * D), D ** -0.5),
        "conv_b_pw1": nrm(ks[7], (N_CONV_LAYERS, 2 * D), 0.02),
        "conv_w_dw": nrm(ks[8], (N_CONV_LAYERS, CONV_WIDTH, D), CONV_WIDTH ** -0.5),
        "conv_b_dw": nrm(ks[9], (N_CONV_LAYERS, D), 0.02),
        "conv_ln_g": 1.0 + nrm(ks[10], (N_CONV_LAYERS, D), 0.02),
        "conv_ln_b": nrm(ks[11], (N_CONV_LAYERS, D), 0.02),
        "conv_w_pw2": nrm(ks[12], (N_CONV_LAYERS, D, D), D ** -0.5),
        "conv_b_pw2": nrm(ks[13], (N_CONV_LAYERS, D), 0.02),
        "ret_w_in": nrm(ks[14], (N_RET_LAYERS, D, RET_IN_WIDTH), D ** -0.5),
        "ret_gn_g": 1.0 + nrm(ks[15], (N_RET_LAYERS, RET_HEADS, RET_V_DIM), 0.02),
        "ret_gn_b": nrm(ks[16], (N_RET_LAYERS, RET_HEADS, RET_V_DIM), 0.02),
        "ret_w_out": nrm(ks[17], (N_RET_LAYERS, RET_V_TOTAL, D), RET_V_TOTAL ** -0.5),
        "mlp_w1": nrm(ks[18], (DEPTH, D, D_FF), D ** -0.5),
        "mlp_w2": nrm(ks[19], (DEPTH, D_FF, D), D_FF ** -0.5),
        "final_norm_g": 1.0 + nrm(ks[20], (D,), 0.02),
    }


def reference(x, c, ada_w, ada_b, norm_mix_g, norm_mlp_g, conv_w_pw1, conv_b_pw1, conv_w_dw,
              conv_b_dw, conv_ln_g, conv_ln_b, conv_w_pw2, conv_b_pw2, ret_w_in, ret_gn_g,
              ret_gn_b, ret_w_out, mlp_w1, mlp_w2, final_norm_g):
    seq = x.shape[1]
    cos, sin = rope_tables(seq)
    log_gamma = jnp.log(1.0 - 2.0 ** (-5.0 - jnp.arange(RET_HEADS, dtype=jnp.float32)))
    cond = jax.nn.silu(c)
    for i in range(DEPTH):
        mod = cond @ ada_w[i] + ada_b[i]
        sh1, sc1, g1, sh2, sc2, g2 = jnp.split(mod, 6, axis=-1)
        h = modulate(rmsnorm(x, norm_mix_g[i]), sh1, sc1)
        j = i // N_MIXERS
        if i % N_MIXERS == 0:
            y = conformer_conv(h, conv_w_pw1[j], conv_b_pw1[j], conv_w_dw[j], conv_b_dw[j],
                               conv_ln_g[j], conv_ln_b[j], conv_w_pw2[j], conv_b_pw2[j])
        else:
            y = retention(h, ret_w_in[j], ret_gn_g[j], ret_gn_b[j], ret_w_out[j],
                          cos, sin, log_gamma)
        x = x + g1[:, None, :] * y
        h = modulate(rmsnorm(x, norm_mlp_g[i]), sh2, sc2)
        x = x + g2[:, None, :] * (jnp.square(jax.nn.relu(h @ mlp_w1[i])) @ mlp_w2[i])
    return rmsnorm(x, final_norm_g)
```

```python
import contextlib
import math
import numpy as np
import concourse.bass as bass
import concourse.mybir as mybir
from concourse.bass_utils import run_bass_kernel_spmd

F32 = mybir.dt.float32
BF16 = mybir.dt.bfloat16
I32 = mybir.dt.int32
ALU = mybir.AluOpType
AF = mybir.ActivationFunctionType

BLK = 64


class Buf:
    def __init__(self, name, t, nelem):
        self.name = name
        self.t = t
        self.n = (nelem + BLK - 1) // BLK
        self.lastw = [None] * self.n
        self.readers = [[] for _ in range(self.n)]

    def rg(self, lo=0, hi=None):
        if hi is None:
            hi = self.n * BLK
        return (self, lo // BLK, (hi + BLK - 1) // BLK)


class Op:
    __slots__ = ("eng", "fn", "deps", "is_async", "key", "inc", "signal", "sigval", "waits", "idx")

    def __init__(self, eng, fn, is_async, key, inc):
        self.eng = eng
        self.fn = fn
        self.deps = set()
        self.is_async = is_async
        self.key = key
        self.inc = inc
        self.signal = is_async
        self.sigval = None
        self.waits = []


class Prog:
    ENGS = ("pe", "act", "dve", "pool", "sp")

    def __init__(self, nc, same_engine_sync=True):
        self.nc = nc
        self.ops = []
        self.same = same_engine_sync
        self.async_counts = {}

    def add(self, eng, fn, reads=(), writes=(), async_key=None, inc=16):
        op = Op(eng, fn, async_key is not None, async_key, inc)
        op.idx = len(self.ops)
        for (b, lo, hi) in reads:
            for i in range(lo, hi):
                w = b.lastw[i]
                if w is not None:
                    op.deps.add(w)
        for (b, lo, hi) in writes:
            for i in range(lo, hi):
                w = b.lastw[i]
                if w is not None:
                    op.deps.add(w)
                for r in b.readers[i]:
                    op.deps.add(r)
        for (b, lo, hi) in reads:
            for i in range(lo, hi):
                b.readers[i].append(op)
        for (b, lo, hi) in writes:
            for i in range(lo, hi):
                b.lastw[i] = op
                b.readers[i] = []
        op.deps.discard(op)
        if op.is_async:
            c = self.async_counts.get(async_key, 0) + inc
            self.async_counts[async_key] = c
            op.sigval = c
        self.ops.append(op)
        return op

    def _needs_wait(self, op, d):
        if d.is_async:
            return True
        if d.eng != op.eng:
            return True
        if op.is_async:
            return True
        if op.eng == "pe":
            return False
        return self.same

    def finalize_and_emit(self):
        nc = self.nc
        for op in self.ops:
            op.deps = [d for d in op.deps if self._needs_wait(op, d)]
            for d in op.deps:
                d.signal = True
        cnt = {e: 0 for e in self.ENGS}
        for op in self.ops:
            if not op.is_async and op.signal:
                cnt[op.eng] += 1
                op.sigval = cnt[op.eng]
        waited = {e: {} for e in self.ENGS}
        for op in self.ops:
            need = {}
            for d in op.deps:
                k = ("A", d.key) if d.is_async else ("E", d.eng)
                if d.sigval > need.get(k, 0):
                    need[k] = d.sigval
            w = waited[op.eng]
            for k, v in need.items():
                if w.get(k, 0) < v:
                    w[k] = v
                    op.waits.append((k, v))
        with contextlib.ExitStack() as es:
            sems = {}
            for e in ("pe", "act", "dve", "pool"):
                sems[("E", e)] = es.enter_context(nc.semaphore("s_" + e))
            for k in self.async_counts:
                sems[("A", k)] = es.enter_context(nc.semaphore("a_" + str(k)))
            block = es.enter_context(nc.Block())
            per = {e: [op for op in self.ops if op.eng == e] for e in self.ENGS}

            def run(engobj, lst, ename):
                for op in lst:
                    for (k, v) in op.waits:
                        engobj.wait_ge(sems[k], v)
                    ins = op.fn(engobj)
                    if op.is_async:
                        if op.inc == 16:
                            ins.then_inc(sems[("A", op.key)], 16)
                        else:
                            ins.then_inc(sems[("A", op.key)])
                    elif op.signal:
                        ins.then_inc(sems[("E", ename)], 1)
                last = {}
                for op in lst:
                    if op.is_async:
                        last[op.key] = max(last.get(op.key, 0), op.sigval)
                for k, v in last.items():
                    engobj.wait_ge(sems[("A", k)], v)

            @block.tensor
            def _(e):
                run(e, per["pe"], "pe")

            @block.scalar
            def _(e):
                run(e, per["act"], "act")

            @block.vector
            def _(e):
                run(e, per["dve"], "dve")

            @block.gpsimd
            def _(e):
                run(e, per["pool"], "pool")

            @block.sync
            def _(e):
                run(e, per["sp"], "sp")


PIPE_SKEW = 1
D = 1024
NT = 4096
T = 512
NG = NT // T
DEPTH = 4
EPS = 1e-6
HEADS = 4
LG = [math.log(1.0 - 2.0 ** (-5.0 - h)) for h in range(HEADS)]
LN16 = math.log(1.0 / 16.0)

VC_NMIX = 0
VC_NMLP = 32
VC_FIN = 64
VC_BPW1 = 72
VC_BDW = 104
VC_LNG = 120
VC_LNB = 136
VC_BPW2 = 152
VC_WDW = 168
NV = 168 + 2 * 8 * 31


def build(stop_after=None, debug_dump=False):
    nc = bass.Bass("TRN2", target_bir_lowering=False)

    def din(name, shape, dt=F32):
        return nc.dram_tensor(name, shape, dt, kind="ExternalInput").ap()

    x_in = din("x_sh", [NT, D])
    cvec = din("cvec", [128, 8])
    flag_in = din("flag", [128, 1])
    pos_in = din("pos", [128, 32])
    vec_in = din("vec", [128, NV])
    gn_in = din("gnrep", [2 * 2 * 4 * 128, 512])
    ada_w = din("ada_w", [DEPTH * D, 6 * D])
    ada_b = din("ada_b", [1, DEPTH * 6 * D])
    w_pw1 = din("conv_w_pw1", [2 * D, 2 * D])
    w_pw2 = din("conv_w_pw2", [2 * D, D])
    w_rin = din("ret_w_in", [2 * D, 6 * D])
    w_rout = din("ret_w_out", [2 * 2 * D, D])
    w_m1 = din("mlp_w1", [DEPTH * D, 4 * D])
    w_m2 = din("mlp_w2", [DEPTH * 4 * D, D])
    out = nc.dram_tensor("out", [NT, D], F32, kind="ExternalOutput").ap()
    dbg = nc.dram_tensor("dbg", [8 * D, 1024], F32, kind="ExternalOutput").ap() if debug_dump else None

    xs_t = nc.dram_tensor("xs", [D, NT], F32)
    hs_t = nc.dram_tensor("hs", [D, NT], BF16)
    tab_t = nc.dram_tensor("tabs", [32 * 128, 1024], F32)
    kv_t = nc.dram_tensor("kvs", [32 * 128, 3072], BF16)
    hb_t = [nc.dram_tensor("halo_b%d" % i, [1024, 32], F32) for i in range(2)]
    hg_t = [nc.dram_tensor("halo_g%d" % i, [2048, 32], F32) for i in range(2)]
    sb_t = [nc.dram_tensor("st_b%d" % i, [1024, 512], F32) for i in range(2)]
    sg_t = [nc.dram_tensor("st_g%d" % i, [2048, 512], F32) for i in range(2)]
    xs, hs, tabs = xs_t.ap(), hs_t.ap(), tab_t.ap()
    xs_v = xs.rearrange("(k p) t -> p k t", p=128)
    hs_v = hs.rearrange("(k p) t -> p k t", p=128)
    PAIRS = [[0, 1], [2, 3], [4, 5], [6, 7]]

    with contextlib.ExitStack() as es:
        def sb(name, n, dt):
            t = es.enter_context(nc.sbuf_tensor(name, [128, n], dt))
            return Buf(name, t, n)

        def psb(name, n, dt):
            t = es.enter_context(nc.psum_tensor(name, [128, n], dt))
            return Buf(name, t, n)

        WA = sb("WA", 32768, BF16)
        WB = sb("WB", 32768, BF16)
        XT = sb("XT", 4096, F32)
        H = sb("H", 4096, BF16)
        BFA = sb("BFA", 8192, BF16)
        FA = sb("FA", 4096, F32)
        FB = sb("FB", 3072, F32)
        VEC = sb("VEC", NV, F32)
        MODV = sb("MODV", DEPTH * 48, F32)
        IDF = sb("IDF", 128, F32)
        IDB = sb("IDB", 128, BF16)
        ONES = sb("ONES", 128, BF16)
        CST = sb("CST", 16, F32)
        MASK = sb("MASK", 512, F32)
        XI = sb("XI", 1024, F32)
        SM = sb("SM", 64, F32)
        CB = sb("CB", 8, BF16)
        II = sb("II", 256, I32)
        PS = [psb("PS%d" % i, 512, F32) for i in range(6)]
        PB = [psb("PB%d" % i, 1024, BF16) for i in range(2)]
        XSB = Buf("xs", xs_t, NG * BLK)
        HSB = Buf("hs", hs_t, NG * BLK)
        TABB = Buf("tabs", tab_t, 32 * BLK)
        KVB = Buf("kvs", kv_t, 32 * BLK)
        HBB = [Buf("hb%d" % i, hb_t[i], BLK) for i in range(2)]
        HGB = [Buf("hg%d" % i, hg_t[i], BLK) for i in range(2)]
        SBB = [Buf("sbb%d" % i, sb_t[i], BLK) for i in range(2)]
        SGB = [Buf("sgb%d" % i, sg_t[i], BLK) for i in range(2)]
        OUTB = Buf("out", None, 32 * BLK)

        p = Prog(nc)
        st = {"ps": 0, "fb": 0, "dk": 0}

        def next_ps():
            b = PS[st["ps"] % 6]
            st["ps"] += 1
            return b

        def dkey(prefix):
            st["dk"] += 1
            return "%s%d" % (prefix, st["dk"] % 4)

        def mm(ps_buf, plo, n, lhsT, lrg, rhs, rrg, start, stop, m=128):
            o = ps_buf.t[0:m, plo:plo + n]
            p.add("pe", lambda e: e.matmul(o, lhsT=lhsT, rhs=rhs, start=start, stop=stop),
                  reads=[lrg, rrg], writes=[ps_buf.rg(plo, plo + n)])

        def act(out_ap, in_ap, func, reads, writes, bias=None, scale=None):
            kw = {}
            if bias is not None:
                kw["bias"] = bias
            if scale is not None:
                kw["scale"] = scale
            p.add("act", lambda e: e.activation(out=out_ap, in_=in_ap, func=func, **kw), reads=reads, writes=writes)

        def tt(eng, out_ap, a, b, op, reads, writes):
            p.add(eng, lambda e: e.tensor_tensor(out=out_ap, in0=a, in1=b, op=op), reads=reads, writes=writes)

        def ts(eng, out_ap, a, s1, op0, reads, writes, s2=None, op1=None):
            if op1 is None:
                p.add(eng, lambda e: e.tensor_scalar(out=out_ap, in0=a, scalar1=s1, scalar2=None, op0=op0),
                      reads=reads, writes=writes)
            else:
                p.add(eng, lambda e: e.tensor_scalar(out=out_ap, in0=a, scalar1=s1, scalar2=s2, op0=op0, op1=op1),
                      reads=reads, writes=writes)

        def stt(out_ap, a, s, b, op0, op1, reads, writes):
            p.add("dve", lambda e: e.scalar_tensor_tensor(out=out_ap, in0=a, scalar=s, in1=b, op0=op0, op1=op1),
                  reads=reads, writes=writes)

        def cp(eng, out_ap, in_ap, reads, writes):
            if eng == "act":
                act(out_ap, in_ap, AF.Copy, reads, writes)
            else:
                p.add(eng, lambda e: e.tensor_copy(out=out_ap, in_=in_ap), reads=reads, writes=writes)

        def dma(eng, out_ap, in_ap, reads, writes, key):
            p.add(eng, lambda e: e.dma_start(out=out_ap, in_=in_ap), reads=reads, writes=writes, async_key=key)

        def fb_slot():
            s = st["fb"] % 6
            st["fb"] += 1
            return s * 512

        def recip(buf, lo, n):
            ap_ = buf.t[:, lo:lo + n]
            p.add("dve", lambda e: e.reciprocal(out=ap_, in_=ap_), reads=[buf.rg(lo, lo + n)], writes=[buf.rg(lo, lo + n)])

        def bnstats(pbuf):
            src_ = pbuf.t[:]
            p.add("dve", lambda e: e.bn_stats(out=SM.t[:, 32:38], in_=src_), reads=[pbuf.rg()], writes=[SM.rg(32, 38)])
            p.add("dve", lambda e: e.bn_aggr(out=SM.t[:, 40:42], in_=SM.t[:, 32:38]), reads=[SM.rg(32, 38)], writes=[SM.rg(40, 42)])

        EPS_AP = CST.t[:, 0:1]
        FLAG_AP = CST.t[:, 1:2]

        dma("sp", VEC.t[:], vec_in, [], [VEC.rg()], "c0")
        dma("sp", CST.t[:, 1:2], flag_in, [], [CST.rg()], "c1")
        p.add("dve", lambda e: e.memset(CST.t[:, 0:1], EPS), writes=[CST.rg()])
        p.add("dve", lambda e: e.memset(CST.t[:, 6:7], LN16), writes=[CST.rg()])
        p.add("dve", lambda e: e.memset(ONES.t[:], 1.0 / 1024.0), writes=[ONES.rg()])
        p.add("pool", lambda e: e.iota(II.t[:, 0:128], pattern=[[1, 128]], base=0, channel_multiplier=-1), writes=[II.rg()])
        cp("dve", FB.t[:, 0:128], II.t[:, 0:128], [II.rg()], [FB.rg(0, 128)])
        ts("dve", IDF.t[:], FB.t[:, 0:128], 0.0, ALU.is_equal, [FB.rg(0, 128)], [IDF.rg()])
        cp("dve", IDB.t[:], IDF.t[:], [IDF.rg()], [IDB.rg()])
        ts("dve", FB.t[:, 128:256], FB.t[:, 0:128], -1.0, ALU.mult, [FB.rg(0, 128)], [FB.rg(128, 256)])
        tt("dve", FB.t[:, 0:128], FB.t[:, 0:128], FB.t[:, 128:256], ALU.max, [FB.rg(0, 256)], [FB.rg(0, 128)])
        for h in range(HEADS):
            act(MASK.t[:, h * 128:(h + 1) * 128], FB.t[:, 0:128], AF.Exp, [FB.rg(0, 128), CST.rg()], [MASK.rg(h * 128, (h + 1) * 128)],
                bias=CST.t[:, 6:7], scale=LG[h])
        p.add("dve", lambda e: e.memset(MASK.t[64:128, :].rearrange("p (h c) -> p h c", h=4)[:, :, 0:64], 0.0),
              reads=[MASK.rg()], writes=[MASK.rg()])
        p.add("pool", lambda e: e.iota(II.t[:, 0:128], pattern=[[1, 128]], base=1, channel_multiplier=0), reads=[II.rg()], writes=[II.rg()])
        cp("dve", FB.t[:, 256:384], II.t[:, 0:128], [II.rg()], [FB.rg(256, 384)])
        for h in range(HEADS):
            for r in range(2):
                o = h * 256 + r * 128
                act(XI.t[:, o:o + 128], FB.t[:, 256:384], AF.Exp, [FB.rg(256, 384)], [XI.rg(o, o + 128)], scale=LG[h])
        p.add("pool", lambda e: e.iota(II.t[:, 0:1], pattern=[[0, 1]], base=127, channel_multiplier=-1), reads=[II.rg()], writes=[II.rg()])
        cp("dve", FB.t[:, 384:385], II.t[:, 0:1], [II.rg()], [FB.rg(384, 385)])
        for h in range(HEADS):
            act(CST.t[:, 2 + h:3 + h], FB.t[:, 384:385], AF.Exp, [FB.rg(384, 385), CST.rg()], [CST.rg()],
                bias=CST.t[:, 6:7], scale=LG[h])
        DEC = [math.exp(LG[h] * 128.0) for h in range(HEADS)]

        dma("sp", SM.t[:, 48:56], cvec, [], [SM.rg(48, 56)], "c3")
        act(CB.t[:], SM.t[:, 48:56], AF.Silu, [SM.rg(48, 56)], [CB.rg()])
        ROW = XT
        def ada_block(i, half):
            dma("sp", FB.t[0:1, 0:3072], ada_b[0:1, i * 6144 + half * 3072:i * 6144 + (half + 1) * 3072], [],
                [FB.rg(0, 3072)], dkey("ab"))
            for ct in range(6):
                col = (half * 6 + ct) * 512
                wb = WA if ct % 2 == 0 else WB
                wo = (ct // 2) * 4096
                src = ada_w[i * D:(i + 1) * D, col:col + 512].rearrange("(k p) n -> p k n", p=128)
                dst = wb.t[:, wo:wo + 4096].rearrange("p (k n) -> p k n", k=8)
                dma("pool", dst, src, [], [wb.rg(wo, wo + 4096)], dkey("aw"))
                pb = PS[ct]
                for k in range(8):
                    mm(pb, 0, 512, CB.t[:, k:k + 1], CB.rg(), wb.t[:, wo + k * 512:wo + (k + 1) * 512],
                       wb.rg(wo + k * 512, wo + (k + 1) * 512), k == 0, False, m=1)
                mm(pb, 0, 512, IDF.t[0:1, 0:1], IDF.rg(), FB.t[0:1, ct * 512:(ct + 1) * 512], FB.rg(ct * 512, (ct + 1) * 512),
                   False, True, m=1)
                cp("act", XT.t[0:1, ct * 512:(ct + 1) * 512], pb.t[0:1, :], [pb.rg()], [XT.rg(ct * 512, (ct + 1) * 512)])
            pb = PS[0]
            for j in range(24):
                o = pb.t[:, j:j + 1]
                lhsT = XT.t[0:1, j * 128:(j + 1) * 128]
                rhs = IDF.t[0:1, 0:1]
                p.add("pe", (lambda o, lhsT, rhs: lambda e: e.matmul(o, lhsT=lhsT, rhs=rhs, start=True, stop=True))(o, lhsT, rhs),
                      reads=[XT.rg(j * 128, (j + 1) * 128), IDF.rg()], writes=[pb.rg(0, 24)])
            cp("act", MODV.t[:, i * 48 + half * 24:i * 48 + (half + 1) * 24], pb.t[:, 0:24], [pb.rg(0, 24)],
               [MODV.rg(i * 48 + half * 24, i * 48 + (half + 1) * 24)])

        POS = FA
        dma("sp", FA.t[:, 0:32], pos_in, [], [FA.rg(0, 32)], "c2")
        p.add("dve", lambda e: e.memset(FA.t[:, 128:129], 1.0), writes=[FA.rg(128, 129)])
        rr = 10000.0 ** (-2.0 / 256.0)
        for s in range(7):
            n = 1 << s
            ts("dve", FA.t[:, 128 + n:128 + 2 * n], FA.t[:, 128:128 + n], float(np.float32(rr ** n)), ALU.mult,
               [FA.rg(128, 256)], [FA.rg(128, 256)])
        TWO_PI = 2.0 * math.pi
        def rope_tile(tI):
            a = 512 + (tI % 2) * 1792
            ANG = FA.t[:, a:a + 256]
            NN = FA.t[:, a + 256:a + 512]
            TAB = a + 768
            rga = FA.rg(a, a + 1792)
            ts("dve", FA.t[:, a + 128:a + 256], FA.t[:, 128:256], FA.t[:, tI:tI + 1], ALU.mult, [FA.rg(0, 256)], [rga])
            ts("dve", FA.t[:, a:a + 128], FA.t[:, a + 128:a + 256], math.pi / 2.0, ALU.add, [rga], [rga])
            ts("dve", NN, ANG, 1.0 / TWO_PI, ALU.mult, [rga], [rga])
            cp("dve", II.t[:], NN, [rga], [II.rg()])
            cp("dve", NN, II.t[:], [II.rg()], [rga])
            stt(ANG, NN, -6.28125, ANG, ALU.mult, ALU.add, [rga], [rga])
            stt(ANG, NN, -0.0019353071795864769, ANG, ALU.mult, ALU.add, [rga], [rga])
            ts("dve", NN, ANG, math.pi, ALU.is_gt, [rga], [rga], s2=TWO_PI, op1=ALU.mult)
            tt("dve", ANG, ANG, NN, ALU.subtract, [rga], [rga])
            ts("dve", ANG, ANG, math.pi, ALU.min, [rga], [rga], s2=-math.pi, op1=ALU.max)
            act(FA.t[:, a + 512:a + 768], ANG, AF.Sin, [rga], [rga])
            COS = FA.t[:, a + 512:a + 640]
            SIN = FA.t[:, a + 640:a + 768]
            for r in range(4):
                cp("act" if r % 2 else "pool", FA.t[:, TAB + r * 128:TAB + (r + 1) * 128], COS, [rga], [rga])
            for r in range(4):
                o = TAB + 512 + r * 128
                if r % 2 == 0:
                    ts("dve", FA.t[:, o:o + 128], SIN, -1.0, ALU.mult, [rga], [rga])
                else:
                    cp("pool", FA.t[:, o:o + 128], SIN, [rga], [rga])
            dma("sp", tabs[tI * 128:(tI + 1) * 128, :], FA.t[:, TAB:TAB + 1024], [rga], [TABB.rg(tI * BLK, (tI + 1) * BLK)], dkey("tb"))

        for blk in range(8):
            ada_block(blk // 2, blk % 2)
            for q4 in range(4):
                rope_tile(blk * 4 + q4)

        def modcol(i, j):
            o = i * 48 + j * 8
            return MODV.t[:, o:o + 8], MODV.rg(o, o + 8)

        for g in range(NG):
            src = x_in[g * T:(g + 1) * T, :].rearrange("(t p) d -> p t d", p=128)
            dma("sp", FA.t[:].rearrange("p (t d) -> p t d", t=4), src, [], [FA.rg()], dkey("xi"))
            for c in range(8):
                pb = next_ps()
                for t4 in range(4):
                    o = pb.t[:, t4 * 128:(t4 + 1) * 128]
                    i_ = FA.t[:, t4 * 1024 + c * 128:t4 * 1024 + (c + 1) * 128]
                    p.add("pe", (lambda o, i_: lambda e: e.transpose(o, i_, IDF.t[:]))(o, i_),
                          reads=[FA.rg(t4 * 1024 + c * 128, t4 * 1024 + (c + 1) * 128), IDF.rg()],
                          writes=[pb.rg(t4 * 128, (t4 + 1) * 128)])
                cp("act" if c % 2 else "dve", XT.t[:, c * 512:(c + 1) * 512], pb.t[:], [pb.rg()], [XT.rg(c * 512, (c + 1) * 512)])
            dma("sp", xs_v[:, :, g * T:(g + 1) * T], XT.t[:].rearrange("p (k t) -> p k t", k=8), [XT.rg()], [XSB.rg(g * BLK, (g + 1) * BLK)], dkey("xo"))

        def load_x(g):
            dma("sp", XT.t[:].rearrange("p (k t) -> p k t", k=8), xs_v[:, :, g * T:(g + 1) * T],
                [XSB.rg(g * BLK, (g + 1) * BLK)], [XT.rg()], dkey("xl"))

        def load_x_chunk(g, c):
            dma("sp", XT.t[:, c * T:(c + 1) * T], xs[c * 128:(c + 1) * 128, g * T:(g + 1) * T],
                [XSB.rg(g * BLK, (g + 1) * BLK)], [XT.rg(c * T, (c + 1) * T)], dkey("xl"))

        def store_x_chunk(g, c):
            dma("sp", xs[c * 128:(c + 1) * 128, g * T:(g + 1) * T], XT.t[:, c * T:(c + 1) * T],
                [XT.rg(c * T, (c + 1) * T)], [XSB.rg(g * BLK, (g + 1) * BLK)], dkey("xo"))

        def store_x(g, eng="sp"):
            dma(eng, xs_v[:, :, g * T:(g + 1) * T], XT.t[:].rearrange("p (k t) -> p k t", k=8),
                [XT.rg()], [XSB.rg(g * BLK, (g + 1) * BLK)], dkey("xo"))

        def prep_mod(i, which):
            gcol = (VC_NMIX if which == 0 else VC_NMLP) + i * 8
            sh, shr = modcol(i, 3 * which + 0)
            sc, scr = modcol(i, 3 * which + 1)
            gt, gtr = modcol(i, 3 * which + 2)
            stt(SM.t[:, 0:8], sc, 1.0, VEC.t[:, gcol:gcol + 8], ALU.add, ALU.mult, [scr, VEC.rg()], [SM.rg(0, 8)])
            cp("dve", SM.t[:, 8:16], sh, [shr], [SM.rg(8, 16)])
            cp("dve", SM.t[:, 16:24], gt, [gtr], [SM.rg(16, 24)])

        def norm_mod(src_buf, sofs, n, dst_buf, dofs, sq_buf, sq_ofs, a_ofs=0, b_ofs=8, with_b=True):
            pb = next_ps()
            for c in range(8):
                so = sq_ofs + (c % 2) * n
                act(sq_buf.t[:, so:so + n], src_buf.t[:, sofs + c * n:sofs + (c + 1) * n], AF.Square,
                    [src_buf.rg(sofs + c * n, sofs + (c + 1) * n)], [sq_buf.rg(so, so + n)])
                mm(pb, 0, n, ONES.t[:], ONES.rg(), sq_buf.t[:, so:so + n], sq_buf.rg(so, so + n), c == 0, c == 7)
            r0 = fb_slot()
            act(FB.t[:, r0:r0 + n], pb.t[:, 0:n], AF.Sqrt, [pb.rg(0, n), CST.rg()], [FB.rg(r0, r0 + n)], bias=EPS_AP, scale=1.0)
            recip(FB, r0, n)
            for c in range(8):
                t0 = fb_slot()
                while t0 == r0:
                    t0 = fb_slot()
                stt(FB.t[:, t0:t0 + n], src_buf.t[:, sofs + c * n:sofs + (c + 1) * n], SM.t[:, a_ofs + c:a_ofs + c + 1],
                    FB.t[:, r0:r0 + n], ALU.mult, ALU.mult,
                    [src_buf.rg(sofs + c * n, sofs + (c + 1) * n), SM.rg(), FB.rg(r0, r0 + n)], [FB.rg(t0, t0 + n)])
                if with_b:
                    act(dst_buf.t[:, dofs + c * n:dofs + (c + 1) * n], FB.t[:, t0:t0 + n], AF.Identity,
                        [FB.rg(t0, t0 + n), SM.rg()], [dst_buf.rg(dofs + c * n, dofs + (c + 1) * n)],
                        bias=SM.t[:, b_ofs + c:b_ofs + c + 1], scale=1.0)
                else:
                    cp("act", dst_buf.t[:, dofs + c * n:dofs + (c + 1) * n], FB.t[:, t0:t0 + n],
                       [FB.rg(t0, t0 + n)], [dst_buf.rg(dofs + c * n, dofs + (c + 1) * n)])

        def load_w(dst_buf, dofs, src_ap_2d, nk, ncols, key):
            src = src_ap_2d.rearrange("(k p) n -> p k n", p=128)
            for k0 in range(0, nk, 2):
                k1 = min(nk, k0 + 2)
                d = dst_buf.t[:, dofs + k0 * ncols:dofs + k1 * ncols].rearrange("p (k n) -> p k n", k=k1 - k0)
                dma("pool", d, src[:, k0:k1, :], [], [dst_buf.rg(dofs + k0 * ncols, dofs + k1 * ncols)], key)

        def load_w_strided(dst_buf, dofs, dstride, src_ap_2d, nk, ncols, key):
            src = src_ap_2d.rearrange("(k p) n -> p k n", p=128)
            d = dst_buf.t[:, dofs:dofs + nk * dstride].rearrange("p (k n) -> p k n", k=nk)[:, :, 0:ncols]
            dma("pool", d, src, [], [dst_buf.rg(dofs, dofs + nk * dstride)], key)

        def conv_layer(i):
            l = i // 2
            GLW = 544
            prep_mod(i, 0)
            tt("dve", SM.t[:, 24:32], VEC.t[:, VC_BPW2 + l * 8:VC_BPW2 + l * 8 + 8], SM.t[:, 16:24], ALU.mult,
               [VEC.rg(), SM.rg(16, 24)], [SM.rg(24, 32)])
            load_w(WA, 0, w_pw1[l * D:(l + 1) * D, :], 8, 2048, "wa")
            load_w(WA, 16384, w_pw2[l * D:(l + 1) * D, :], 8, 1024, "wa")
            for c in range(8):
                for k in range(31):
                    o = (c * 31 + k) * 128
                    col = VC_WDW + (l * 8 + c) * 31 + k
                    if (c * 31 + k) % 2:
                        ts("dve", WB.t[:, o:o + 128], IDF.t[:], VEC.t[:, col:col + 1], ALU.mult,
                           [IDF.rg(), VEC.rg()], [WB.rg(o, o + 128)])
                    else:
                        act(WB.t[:, o:o + 128], IDF.t[:], AF.Identity, [IDF.rg(), VEC.rg()], [WB.rg(o, o + 128)],
                            scale=VEC.t[:, col:col + 1])

            def glu_for(hbuf, hofs, n, emit_out):
                for c in range(8):
                    pa, pg = next_ps(), next_ps()
                    for k in range(8):
                        mm(pa, 0, n, WA.t[:, k * 2048 + c * 128:k * 2048 + c * 128 + 128], WA.rg(k * 2048 + c * 128, k * 2048 + c * 128 + 128),
                           hbuf.t[:, hofs + k * n:hofs + (k + 1) * n], hbuf.rg(hofs + k * n, hofs + (k + 1) * n), k == 0, k == 7)
                    for k in range(8):
                        o = k * 2048 + 1024 + c * 128
                        mm(pg, 0, n, WA.t[:, o:o + 128], WA.rg(o, o + 128),
                           hbuf.t[:, hofs + k * n:hofs + (k + 1) * n], hbuf.rg(hofs + k * n, hofs + (k + 1) * n), k == 0, k == 7)
                    s0 = fb_slot()
                    bcol = VC_BPW1 + l * 16
                    act(FB.t[:, s0:s0 + n], pg.t[:, 0:n], AF.Sigmoid, [pg.rg(0, n), VEC.rg()], [FB.rg(s0, s0 + n)],
                        bias=VEC.t[:, bcol + 8 + c:bcol + 9 + c], scale=1.0)
                    emit_out(c, pa, s0, VEC.t[:, bcol + c:bcol + c + 1])

            dma("sp", FA.t[:, 0:256].rearrange("p (k t) -> p k t", k=8), xs_v[:, :, NT - 32:NT], [XSB.rg((NG - 1) * BLK, NG * BLK)],
                [FA.rg(0, 256)], dkey("hl"))
            norm_mod(FA, 0, 32, BFA, 7168, BFA, 7680)
            def halo_out(c, pa, s0, bap):
                stt(FA.t[:, 256 + c * 32:256 + (c + 1) * 32], pa.t[:, 0:32], bap, FB.t[:, s0:s0 + 32], ALU.add, ALU.mult,
                    [pa.rg(0, 32), VEC.rg(), FB.rg(s0, s0 + 32)], [FA.rg(256 + c * 32, 256 + (c + 1) * 32)])
            glu_for(BFA, 7168, 32, halo_out)
            xi_ = l
            dma("pool", hb_t[xi_].ap().rearrange("(k p) t -> p k t", p=128), FA.t[:, 256:512].rearrange("p (k t) -> p k t", k=8),
                [FA.rg(256, 512)], [HBB[xi_].rg()], "hx")
            p.add("pool", lambda e: e.collective_compute("AllGather", ALU.bypass, replica_groups=PAIRS,
                                                         ins=[hb_t[xi_].ap().opt()], outs=[hg_t[xi_].ap().opt()]),
                  reads=[HBB[xi_].rg()], writes=[HGB[xi_].rg()], async_key="cc%d" % i, inc=1)
            dma("pool", FA.t[:, 512:768].rearrange("p (k t) -> p k t", k=8), hg_t[xi_].ap()[0:1024, :].rearrange("(k p) t -> p k t", p=128),
                [HGB[xi_].rg()], [FA.rg(512, 768)], "hx")
            ts("dve", BFA.t[:, 0:8 * GLW].rearrange("p (k t) -> p k t", k=8)[:, :, 0:32],
               FA.t[:, 512:768].rearrange("p (k t) -> p k t", k=8), FLAG_AP, ALU.mult,
               [FA.rg(512, 768), CST.rg()], [BFA.rg(0, 8 * GLW)])

            for g in range(NG):
                if g == 0:
                    load_x(g)
                norm_mod(XT, 0, T, H, 0, BFA, 4352)
                def glu_out(c, pa, s0, bap):
                    o = c * GLW + 32
                    stt(BFA.t[:, o:o + T], pa.t[:], bap, FB.t[:, s0:s0 + T], ALU.add, ALU.mult,
                        [pa.rg(), VEC.rg(), FB.rg(s0, s0 + T)], [BFA.rg(o, o + T)])
                glu_for(H, 0, T, glu_out)
                pmean, pmsq = PS[4], PS[5]
                for c in range(8):
                    pc = PS[c % 4]
                    for k in range(31):
                        o = (c * 31 + k) * 128
                        ro = c * GLW + 2 + k
                        mm(pc, 0, T, WB.t[:, o:o + 128], WB.rg(o, o + 128), BFA.t[:, ro:ro + T], BFA.rg(ro, ro + T), k == 0, k == 30)
                    bdw = VEC.t[:, VC_BDW + l * 8 + c:VC_BDW + l * 8 + c + 1]
                    act(FA.t[:, c * T:(c + 1) * T], pc.t[:], AF.Identity, [pc.rg(), VEC.rg()], [FA.rg(c * T, (c + 1) * T)], bias=bdw, scale=1.0)
                    so = 4352 + (c % 2) * T
                    act(BFA.t[:, so:so + T], pc.t[:], AF.Square, [pc.rg(), VEC.rg()], [BFA.rg(so, so + T)], bias=bdw, scale=1.0)
                    uo = 5376 + (c % 2) * T
                    cp("dve", BFA.t[:, uo:uo + T], FA.t[:, c * T:(c + 1) * T], [FA.rg(c * T, (c + 1) * T)], [BFA.rg(uo, uo + T)])
                    mm(pmean, 0, T, ONES.t[:], ONES.rg(), BFA.t[:, uo:uo + T], BFA.rg(uo, uo + T), c == 0, c == 7)
                    mm(pmsq, 0, T, ONES.t[:], ONES.rg(), BFA.t[:, so:so + T], BFA.rg(so, so + T), c == 0, c == 7)
                gv = BFA.t[:, 0:8 * GLW].rearrange("p (k t) -> p k t", k=8)
                cp("pool", gv[:, :, 0:32], gv[:, :, T:T + 32], [BFA.rg(0, 8 * GLW)], [BFA.rg(0, 8 * GLW)])
                mu, rs = fb_slot(), fb_slot()
                cp("act", FB.t[:, mu:mu + T], pmean.t[:], [pmean.rg()], [FB.rg(mu, mu + T)])
                tt("dve", FB.t[:, rs:rs + T], FB.t[:, mu:mu + T], FB.t[:, mu:mu + T], ALU.mult, [FB.rg(mu, mu + T)], [FB.rg(rs, rs + T)])
                tt("dve", FB.t[:, rs:rs + T], pmsq.t[:], FB.t[:, rs:rs + T], ALU.subtract, [pmsq.rg(), FB.rg(rs, rs + T)], [FB.rg(rs, rs + T)])
                ts("dve", FB.t[:, rs:rs + T], FB.t[:, rs:rs + T], 0.0, ALU.max, [FB.rg(rs, rs + T)], [FB.rg(rs, rs + T)])
                act(FB.t[:, rs:rs + T], FB.t[:, rs:rs + T], AF.Sqrt, [FB.rg(rs, rs + T), CST.rg()], [FB.rg(rs, rs + T)], bias=EPS_AP, scale=1.0)
                recip(FB, rs, T)
                for c in range(8):
                    d0 = fb_slot()
                    while d0 in (mu, rs):
                        d0 = fb_slot()
                    tt("dve", FB.t[:, d0:d0 + T], FA.t[:, c * T:(c + 1) * T], FB.t[:, mu:mu + T], ALU.subtract,
                       [FA.rg(c * T, (c + 1) * T), FB.rg(mu, mu + T)], [FB.rg(d0, d0 + T)])
                    tt("dve", FB.t[:, d0:d0 + T], FB.t[:, d0:d0 + T], FB.t[:, rs:rs + T], ALU.mult,
                       [FB.rg(d0, d0 + T), FB.rg(rs, rs + T)], [FB.rg(d0, d0 + T)])
                    act(H.t[:, c * T:(c + 1) * T], FB.t[:, d0:d0 + T], AF.Silu, [FB.rg(d0, d0 + T), VEC.rg()], [H.rg(c * T, (c + 1) * T)],
                        bias=VEC.t[:, VC_LNB + l * 8 + c:VC_LNB + l * 8 + c + 1], scale=VEC.t[:, VC_LNG + l * 8 + c:VC_LNG + l * 8 + c + 1])
                for oc in range(8):
                    pb = next_ps()
                    for k in range(8):
                        o = 16384 + k * 1024 + oc * 128
                        mm(pb, 0, T, WA.t[:, o:o + 128], WA.rg(o, o + 128), H.t[:, k * T:(k + 1) * T], H.rg(k * T, (k + 1) * T), k == 0, k == 7)
                    t0 = fb_slot()
                    while t0 in (mu, rs):
                        t0 = fb_slot()
                    act(FB.t[:, t0:t0 + T], pb.t[:], AF.Identity, [pb.rg(), SM.rg()], [FB.rg(t0, t0 + T)],
                        bias=SM.t[:, 24 + oc:25 + oc], scale=SM.t[:, 16 + oc:17 + oc])
                    tt("pool", XT.t[:, oc * T:(oc + 1) * T], XT.t[:, oc * T:(oc + 1) * T], FB.t[:, t0:t0 + T], ALU.add,
                       [XT.rg(oc * T, (oc + 1) * T), FB.rg(t0, t0 + T)], [XT.rg(oc * T, (oc + 1) * T)])
                for c in range(8):
                    store_x_chunk(g, c)
                    if g + 1 < NG:
                        load_x_chunk(g + 1, c)

        def mlp_layer(i):
            prep_mod(i, 1)
            load_w(WA, 0, w_m1[i * D:(i + 1) * D, :], 8, 4096, "wa")
            load_w(WB, 0, w_m2[i * 4 * D:(i + 1) * 4 * D, :], 32, 1024, "wb")
            for g in range(NG):
                if g == 0:
                    load_x(g)
                norm_mod(XT, 0, T, H, 0, BFA, 0)

                def w1q(q):
                    hb = (q % 2) * 4096
                    for j in range(8):
                        hc = q * 8 + j
                        pb = next_ps()
                        for k in range(8):
                            o = k * 4096 + hc * 128
                            mm(pb, 0, T, WA.t[:, o:o + 128], WA.rg(o, o + 128), H.t[:, k * T:(k + 1) * T], H.rg(k * T, (k + 1) * T), k == 0, k == 7)
                        r0 = fb_slot()
                        act(FB.t[:, r0:r0 + T], pb.t[:], AF.Relu, [pb.rg()], [FB.rg(r0, r0 + T)])
                        tt("dve" if j % 2 else "pool", BFA.t[:, hb + j * T:hb + (j + 1) * T], FB.t[:, r0:r0 + T], FB.t[:, r0:r0 + T], ALU.mult,
                           [FB.rg(r0, r0 + T)], [BFA.rg(hb + j * T, hb + (j + 1) * T)])

                def w2q(q):
                    hb = (q % 2) * 4096
                    for oc in range(8):
                        pb = next_ps()
                        for j in range(8):
                            o = (q * 8 + j) * 1024 + oc * 128
                            mm(pb, 0, T, WB.t[:, o:o + 128], WB.rg(o, o + 128), BFA.t[:, hb + j * T:hb + (j + 1) * T],
                               BFA.rg(hb + j * T, hb + (j + 1) * T), j == 0, j == 7)
                        stt(XT.t[:, oc * T:(oc + 1) * T], pb.t[:], SM.t[:, 16 + oc:17 + oc], XT.t[:, oc * T:(oc + 1) * T], ALU.mult, ALU.add,
                            [pb.rg(), SM.rg(), XT.rg(oc * T, (oc + 1) * T)], [XT.rg(oc * T, (oc + 1) * T)])
                w1q(0)
                w1q(1)
                w2q(0)
                w1q(2)
                w2q(1)
                w1q(3)
                w2q(2)
                w2q(3)
                for c in range(8):
                    store_x_chunk(g, c)
                    if g + 1 < NG:
                        load_x_chunk(g + 1, c)

        def rope_bank(pbank, tab_ofs, out_ofs):
            t1, t2 = out_ofs, fb_slot()
            while t2 in (tab_ofs, tab_ofs + 512, out_ofs):
                t2 = fb_slot()
            tt("dve", FB.t[:, t1:t1 + 512], pbank.t[:], FB.t[:, tab_ofs:tab_ofs + 512], ALU.mult,
               [pbank.rg(), FB.rg(tab_ofs, tab_ofs + 512)], [FB.rg(t1, t1 + 512)])
            pv = pbank.t[:].rearrange("p (h two d) -> p h two d", h=2, two=2)
            sv = FB.t[:, tab_ofs + 512:tab_ofs + 1024].rearrange("p (h two d) -> p h two d", h=2, two=2)
            ov = FB.t[:, t2:t2 + 512].rearrange("p (h two d) -> p h two d", h=2, two=2)
            for a_, b_ in ((0, 1), (1, 0)):
                tt("dve", ov[:, :, a_, :], pv[:, :, b_, :], sv[:, :, a_, :], ALU.mult,
                   [pbank.rg(), FB.rg(tab_ofs + 512, tab_ofs + 1024)], [FB.rg(t2, t2 + 512)])
            tt("dve", FB.t[:, t1:t1 + 512], FB.t[:, t1:t1 + 512], FB.t[:, t2:t2 + 512], ALU.add,
               [FB.rg(t1, t1 + 512), FB.rg(t2, t2 + 512)], [FB.rg(t1, t1 + 512)])

        def ret_layer(i):
            l = i // 2
            prep_mod(i, 0)
            wbase = w_rin[l * D:(l + 1) * D, :]
            load_w(WA, 0, wbase[:, 1024:2048], 8, 1024, "wa")
            load_w(WA, 8192, wbase[:, 2048:4096], 8, 2048, "wa")
            p.add("pool", lambda e: e.memset(FA.t[:], 0.0), writes=[FA.rg()])
            KZ, VV = 0, 1024
            for g in range(NG):
                if g == 0:
                    load_x(g)
                norm_mod(XT, 0, T, H, 0, BFA, 3072)
                if g + 1 < NG:
                    load_x(g + 1)
                dma("pool", hs_v[:, :, g * T:(g + 1) * T], H.t[:].rearrange("p (k t) -> p k t", k=8), [H.rg()],
                    [HSB.rg(g * BLK, (g + 1) * BLK)], dkey("ho"))
                for t4 in range(4):
                    tI = g * 4 + t4
                    tab = 0
                    dma("sp", FB.t[:, 0:1024], tabs[tI * 128:(tI + 1) * 128, :], [TABB.rg(tI * BLK, (tI + 1) * BLK)], [FB.rg(0, 1024)], dkey("tl"))
                    st["fb"] = 2
                    for j in range(2):
                        pb = next_ps()
                        for k in range(8):
                            o = k * 1024 + j * 512
                            mm(pb, 0, 512, H.t[:, k * T + t4 * 128:k * T + (t4 + 1) * 128], H.rg(k * T + t4 * 128, k * T + (t4 + 1) * 128),
                               WA.t[:, o:o + 512], WA.rg(o, o + 512), k == 0, k == 7)
                        ko = 1024 + j * 512
                        st["fb"] = 4
                        rope_bank(pb, 0, ko)
                        for hh in range(2):
                            hd = j * 2 + hh
                            ts("dve", BFA.t[:, KZ + hd * 256:KZ + (hd + 1) * 256], FB.t[:, ko + hh * 256:ko + (hh + 1) * 256],
                               CST.t[:, 2 + hd:3 + hd], ALU.mult, [FB.rg(ko + hh * 256, ko + (hh + 1) * 256), CST.rg()],
                               [BFA.rg(KZ + hd * 256, KZ + (hd + 1) * 256)])
                    for hd in range(4):
                        pb = next_ps()
                        for k in range(8):
                            o = 8192 + k * 2048 + hd * 512
                            mm(pb, 0, 512, H.t[:, k * T + t4 * 128:k * T + (t4 + 1) * 128], H.rg(k * T + t4 * 128, k * T + (t4 + 1) * 128),
                               WA.t[:, o:o + 512], WA.rg(o, o + 512), k == 0, k == 7)
                        cp("act", BFA.t[:, VV + hd * 512:VV + (hd + 1) * 512], pb.t[:], [pb.rg()], [BFA.rg(VV + hd * 512, VV + (hd + 1) * 512)])
                    dma("pool", kv_t.ap()[tI * 128:(tI + 1) * 128, :], BFA.t[:, 0:3072], [BFA.rg(0, 3072)],
                        [KVB.rg(tI * BLK, (tI + 1) * BLK)], dkey("kv"))
                    for hd in range(4):
                        for hf in range(2):
                            pb = next_ps()
                            ko = KZ + hd * 256 + hf * 128
                            mm(pb, 0, 512, BFA.t[:, ko:ko + 128], BFA.rg(ko, ko + 128), BFA.t[:, VV + hd * 512:VV + (hd + 1) * 512],
                               BFA.rg(VV + hd * 512, VV + (hd + 1) * 512), True, True)
                            so = (hd * 2 + hf) * 512
                            stt(FA.t[:, so:so + 512], FA.t[:, so:so + 512], DEC[hd], pb.t[:], ALU.mult, ALU.add,
                                [FA.rg(so, so + 512), pb.rg()], [FA.rg(so, so + 512)])
            dma("pool", sb_t[l].ap().rearrange("(k p) f -> p k f", p=128), FA.t[:].rearrange("p (k f) -> p k f", k=8),
                [FA.rg()], [SBB[l].rg()], "sx")
            p.add("pool", lambda e: e.collective_compute("AllGather", ALU.bypass, replica_groups=PAIRS,
                                                         ins=[sb_t[l].ap().opt()], outs=[sg_t[l].ap().opt()]),
                  reads=[SBB[l].rg()], writes=[SGB[l].rg()], async_key="cc%d" % i, inc=1)

            SLOT = [(WA, 0), (WA, 16384), (WB, 0), (WB, 16384)]

            def load_head_w(hd):
                wb, wo = SLOT[hd]
                load_w_strided(wb, wo, 1536, wbase[:, hd * 256:(hd + 1) * 256], 8, 256, "wh%d" % hd)
                load_w_strided(wb, wo + 256, 1536, wbase[:, 1024 + hd * 256:1024 + (hd + 1) * 256], 8, 256, "wh%d" % hd)
                load_w_strided(wb, wo + 512, 1536, wbase[:, 2048 + hd * 512:2048 + (hd + 1) * 512], 8, 512, "wh%d" % hd)
                load_w_strided(wb, wo + 1024, 1536, wbase[:, 4096 + hd * 512:4096 + (hd + 1) * 512], 8, 512, "wh%d" % hd)
                load_w(wb, wo + 12288, w_rout[l * 2 * D + hd * 512:l * 2 * D + (hd + 1) * 512, :], 4, 1024, "wh%d" % hd)

            load_head_w(0)
            load_head_w(1)
            STB, QKB, QKT, QXT, KZH, SCT, VH, Y2, Y2T = 0, 1024, 1536, 2048, 2304, 2560, 2688, 3200, 3712
            for hd in range(HEADS):
                wb, wo = SLOT[hd]
                if hd + 2 < HEADS:
                    load_head_w(hd + 2)
                dma("sp", FA.t[:, 0:1024].rearrange("p (k f) -> p k f", k=2),
                    sg_t[l].ap()[hd * 256:(hd + 1) * 256, :].rearrange("(k p) f -> p k f", p=128),
                    [SGB[l].rg()], [FA.rg(0, 1024)], dkey("sl"))
                ts("dve", FA.t[:, 0:1024], FA.t[:, 0:1024], FLAG_AP, ALU.mult, [FA.rg(0, 1024), CST.rg()], [FA.rg(0, 1024)])
                cp("act", BFA.t[:, STB:STB + 1024], FA.t[:, 0:1024], [FA.rg(0, 1024)], [BFA.rg(STB, STB + 1024)])
                gofs = ((0 * 2 + l) * 4 + hd) * 128
                bofs = ((1 * 2 + l) * 4 + hd) * 128
                dma("sp", FA.t[:, 1024:1536], gn_in[gofs:gofs + 128, :], [], [FA.rg(1024, 1536)], dkey("gl"))
                dma("sp", FA.t[:, 1536:2048], gn_in[bofs:bofs + 128, :], [], [FA.rg(1536, 2048)], dkey("gl"))
                for g in range(NG):
                    dma("sp", H.t[:].rearrange("p (k t) -> p k t", k=8), hs_v[:, :, g * T:(g + 1) * T],
                        [HSB.rg(g * BLK, (g + 1) * BLK)], [H.rg()], dkey("hl"))
                    for t4 in range(4):
                        tI = g * 4 + t4
                        dma("sp", FB.t[:, 0:1024], tabs[tI * 128:(tI + 1) * 128, :], [TABB.rg(tI * BLK, (tI + 1) * BLK)], [FB.rg(0, 1024)], dkey("tl"))
                        pqk, pg = next_ps(), next_ps()
                        dma("sp", BFA.t[:, VH:VH + 512], kv_t.ap()[tI * 128:(tI + 1) * 128, 1024 + hd * 512:1024 + (hd + 1) * 512],
                            [KVB.rg(tI * BLK, (tI + 1) * BLK)], [BFA.rg(VH, VH + 512)], dkey("kl"))
                        dma("sp", BFA.t[:, KZH:KZH + 256], kv_t.ap()[tI * 128:(tI + 1) * 128, hd * 256:(hd + 1) * 256],
                            [KVB.rg(tI * BLK, (tI + 1) * BLK)], [BFA.rg(KZH, KZH + 256)], dkey("kl"))
                        for (pb, co) in ((pqk, 0), (pg, 1024)):
                            for k in range(8):
                                o = wo + k * 1536 + co
                                mm(pb, 0, 512, H.t[:, k * T + t4 * 128:k * T + (t4 + 1) * 128], H.rg(k * T + t4 * 128, k * T + (t4 + 1) * 128),
                                   wb.t[:, o:o + 512], wb.rg(o, o + 512), k == 0, k == 7)
                        st["fb"] = 5
                        rope_bank(pqk, 0, 1024)
                        cp("act", BFA.t[:, QKB:QKB + 512], FB.t[:, 1024:1536], [FB.rg(1024, 1536)], [BFA.rg(QKB, QKB + 512)])
                        sgo = 1536
                        act(FB.t[:, sgo:sgo + 512], pg.t[:], AF.Silu, [pg.rg()], [FB.rg(sgo, sgo + 512)])
                        ptr = PB[0]
                        for b4 in range(4):
                            o_ = ptr.t[:, b4 * 128:(b4 + 1) * 128]
                            i_ = BFA.t[:, QKB + b4 * 128:QKB + (b4 + 1) * 128]
                            p.add("pe", (lambda o_, i_: lambda e: e.transpose(o_, i_, IDB.t[:]))(o_, i_),
                                  reads=[BFA.rg(QKB + b4 * 128, QKB + (b4 + 1) * 128), IDB.rg()], writes=[ptr.rg(b4 * 128, (b4 + 1) * 128)])
                        cp("act", BFA.t[:, QKT:QKT + 512], ptr.t[:, 0:512], [ptr.rg(0, 512)], [BFA.rg(QKT, QKT + 512)])
                        tt("dve", BFA.t[:, QXT:QXT + 256], ptr.t[:, 0:256], XI.t[:, hd * 256:(hd + 1) * 256], ALU.mult,
                           [ptr.rg(0, 256), XI.rg()], [BFA.rg(QXT, QXT + 256)])
                        psc = next_ps()
                        for hf in range(2):
                            mm(psc, 0, 128, BFA.t[:, QKT + 256 + hf * 128:QKT + 384 + hf * 128], BFA.rg(QKT + 256 + hf * 128, QKT + 384 + hf * 128),
                               BFA.t[:, QKT + hf * 128:QKT + (hf + 1) * 128], BFA.rg(QKT + hf * 128, QKT + (hf + 1) * 128), hf == 0, hf == 1)
                        tt("dve", BFA.t[:, SCT:SCT + 128], psc.t[:, 0:128], MASK.t[:, hd * 128:(hd + 1) * 128], ALU.mult,
                           [psc.rg(0, 128), MASK.rg()], [BFA.rg(SCT, SCT + 128)])
                        py = next_ps()
                        mm(py, 0, 512, BFA.t[:, SCT:SCT + 128], BFA.rg(SCT, SCT + 128), BFA.t[:, VH:VH + 512], BFA.rg(VH, VH + 512), True, False)
                        for hf in range(2):
                            mm(py, 0, 512, BFA.t[:, QXT + hf * 128:QXT + (hf + 1) * 128], BFA.rg(QXT + hf * 128, QXT + (hf + 1) * 128),
                               BFA.t[:, STB + hf * 512:STB + (hf + 1) * 512], BFA.rg(STB + hf * 512, STB + (hf + 1) * 512), False, hf == 1)
                        for hf in range(2):
                            pst = next_ps()
                            mm(pst, 0, 512, BFA.t[:, KZH + hf * 128:KZH + (hf + 1) * 128], BFA.rg(KZH + hf * 128, KZH + (hf + 1) * 128),
                               BFA.t[:, VH:VH + 512], BFA.rg(VH, VH + 512), True, True)
                            stt(FA.t[:, hf * 512:(hf + 1) * 512], FA.t[:, hf * 512:(hf + 1) * 512], DEC[hd], pst.t[:], ALU.mult, ALU.add,
                                [FA.rg(hf * 512, (hf + 1) * 512), pst.rg()], [FA.rg(hf * 512, (hf + 1) * 512)])
                            cp("act", BFA.t[:, STB + hf * 512:STB + (hf + 1) * 512], FA.t[:, hf * 512:(hf + 1) * 512],
                               [FA.rg(hf * 512, (hf + 1) * 512)], [BFA.rg(STB + hf * 512, STB + (hf + 1) * 512)])
                        bnstats(py)
                        act(SM.t[:, 42:43], SM.t[:, 41:42], AF.Sqrt, [SM.rg(40, 42), CST.rg()], [SM.rg(42, 43)], bias=EPS_AP, scale=1.0)
                        recip(SM, 42, 1)
                        stt(SM.t[:, 43:44], SM.t[:, 40:41], -1.0, SM.t[:, 42:43], ALU.mult, ALU.mult, [SM.rg(40, 43)], [SM.rg(43, 44)])
                        yno = 2048
                        act(FB.t[:, yno:yno + 512], py.t[:], AF.Identity, [py.rg(), SM.rg(42, 44)], [FB.rg(yno, yno + 512)],
                            bias=SM.t[:, 43:44], scale=SM.t[:, 42:43])
                        tt("dve", FB.t[:, yno:yno + 512], FB.t[:, yno:yno + 512], FA.t[:, 1024:1536], ALU.mult,
                           [FB.rg(yno, yno + 512), FA.rg(1024, 1536)], [FB.rg(yno, yno + 512)])
                        tt("dve", FB.t[:, yno:yno + 512], FB.t[:, yno:yno + 512], FA.t[:, 1536:2048], ALU.add,
                           [FB.rg(yno, yno + 512), FA.rg(1536, 2048)], [FB.rg(yno, yno + 512)])
                        tt("dve", BFA.t[:, Y2:Y2 + 512], FB.t[:, yno:yno + 512], FB.t[:, sgo:sgo + 512], ALU.mult,
                           [FB.rg(yno, yno + 512), FB.rg(sgo, sgo + 512)], [BFA.rg(Y2, Y2 + 512)])
                        ptr2 = PB[1]
                        for b4 in range(4):
                            o_ = ptr2.t[:, b4 * 128:(b4 + 1) * 128]
                            i_ = BFA.t[:, Y2 + b4 * 128:Y2 + (b4 + 1) * 128]
                            p.add("pe", (lambda o_, i_: lambda e: e.transpose(o_, i_, IDB.t[:]))(o_, i_),
                                  reads=[BFA.rg(Y2 + b4 * 128, Y2 + (b4 + 1) * 128), IDB.rg()], writes=[ptr2.rg(b4 * 128, (b4 + 1) * 128)])
                        yv = BFA.t[:, Y2T:Y2T + 2048].rearrange("p (k t) -> p k t", k=4)[:, :, t4 * 128:(t4 + 1) * 128]
                        cp("act", yv, ptr2.t[:, 0:512].rearrange("p (k t) -> p k t", k=4), [ptr2.rg(0, 512)], [BFA.rg(Y2T, Y2T + 2048)])
                    load_x(g)
                    for oc in range(8):
                        pb = next_ps()
                        for kk in range(4):
                            o = wo + 12288 + kk * 1024 + oc * 128
                            mm(pb, 0, T, wb.t[:, o:o + 128], wb.rg(o, o + 128), BFA.t[:, Y2T + kk * 512:Y2T + (kk + 1) * 512],
                               BFA.rg(Y2T + kk * 512, Y2T + (kk + 1) * 512), kk == 0, kk == 3)
                        stt(XT.t[:, oc * T:(oc + 1) * T], pb.t[:], SM.t[:, 16 + oc:17 + oc], XT.t[:, oc * T:(oc + 1) * T], ALU.mult, ALU.add,
                            [pb.rg(), SM.rg(16, 24), XT.rg(oc * T, (oc + 1) * T)], [XT.rg(oc * T, (oc + 1) * T)])
                    store_x(g, "pool")

        def final_pass():
            cp("dve", SM.t[:, 0:8], VEC.t[:, VC_FIN:VC_FIN + 8], [VEC.rg()], [SM.rg(0, 8)])
            for g in range(NG):
                if g == 0:
                    load_x(g)
                pb = next_ps()
                for c in range(8):
                    so = (c % 2) * T
                    act(BFA.t[:, so:so + T], XT.t[:, c * T:(c + 1) * T], AF.Square, [XT.rg(c * T, (c + 1) * T)], [BFA.rg(so, so + T)])
                    mm(pb, 0, T, ONES.t[:], ONES.rg(), BFA.t[:, so:so + T], BFA.rg(so, so + T), c == 0, c == 7)
                r0 = 2560
                act(FB.t[:, r0:r0 + T], pb.t[:], AF.Sqrt, [pb.rg(), CST.rg()], [FB.rg(r0, r0 + T)], bias=EPS_AP, scale=1.0)
                recip(FB, r0, T)
                for c in range(8):
                    stt(FA.t[:, c * T:(c + 1) * T], XT.t[:, c * T:(c + 1) * T], SM.t[:, c:c + 1], FB.t[:, r0:r0 + T], ALU.mult, ALU.mult,
                        [XT.rg(c * T, (c + 1) * T), SM.rg(0, 8), FB.rg(r0, r0 + T)], [FA.rg(c * T, (c + 1) * T)])
                if g + 1 < NG:
                    load_x(g + 1)
                for t4 in range(4):
                    oo = (t4 % 2) * 1024
                    for half in range(2):
                        pt = next_ps()
                        for c4 in range(4):
                            c = half * 4 + c4
                            o_ = pt.t[:, c4 * 128:(c4 + 1) * 128]
                            i_ = FA.t[:, c * T + t4 * 128:c * T + (t4 + 1) * 128]
                            p.add("pe", (lambda o_, i_: lambda e: e.transpose(o_, i_, IDF.t[:]))(o_, i_),
                                  reads=[FA.rg(c * T + t4 * 128, c * T + (t4 + 1) * 128), IDF.rg()], writes=[pt.rg(c4 * 128, (c4 + 1) * 128)])
                        cp("act" if half else "dve", FB.t[:, oo + half * 512:oo + (half + 1) * 512], pt.t[:], [pt.rg()],
                           [FB.rg(oo + half * 512, oo + (half + 1) * 512)])
                    r = g * T + t4 * 128
                    dma("sp", out[r:r + 128, :], FB.t[:, oo:oo + 1024], [FB.rg(oo, oo + 1024)], [OUTB.rg((g * 4 + t4) * BLK, (g * 4 + t4 + 1) * BLK)], dkey("oo"))

        stages = []
        for i in range(DEPTH):
            stages.append(("mix", i))
            stages.append(("mlp", i))
        n = len(stages) if stop_after is None else stop_after
        DBGB = Buf("dbg", None, 16 * BLK)
        for si, (kind, i) in enumerate(stages[:n]):
            if kind == "mix":
                if i % 2 == 0:
                    conv_layer(i)
                else:
                    ret_layer(i)
            else:
                mlp_layer(i)
            if debug_dump:
                dma("sp", dbg[si * D:(si + 1) * D, 0:512], xs[:, 0:512], [XSB.rg(0, BLK)], [DBGB.rg(2 * si * BLK, (2 * si + 1) * BLK)], dkey("dg"))
                dma("sp", dbg[si * D:(si + 1) * D, 512:1024], xs[:, NT - 512:NT], [XSB.rg((NG - 1) * BLK, NG * BLK)],
                    [DBGB.rg((2 * si + 1) * BLK, (2 * si + 2) * BLK)], dkey("dg"))
        if stop_after is None:
            final_pass()
        else:
            for g in range(NG):
                load_x(g)
                for t4 in range(4):
                    oo = (t4 % 2) * 1024
                    for half in range(2):
                        pt = next_ps()
                        for c4 in range(4):
                            c = half * 4 + c4
                            o_ = pt.t[:, c4 * 128:(c4 + 1) * 128]
                            i_ = XT.t[:, c * T + t4 * 128:c * T + (t4 + 1) * 128]
                            p.add("pe", (lambda o_, i_: lambda e: e.transpose(o_, i_, IDF.t[:]))(o_, i_),
                                  reads=[XT.rg(c * T + t4 * 128, c * T + (t4 + 1) * 128), IDF.rg()], writes=[pt.rg(c4 * 128, (c4 + 1) * 128)])
                        cp("act" if half else "dve", FB.t[:, oo + half * 512:oo + (half + 1) * 512], pt.t[:], [pt.rg()],
                           [FB.rg(oo + half * 512, oo + (half + 1) * 512)])
                    r = g * T + t4 * 128
                    dma("sp", out[r:r + 128, :], FB.t[:, oo:oo + 1024], [FB.rg(oo, oo + 1024)], [OUTB.rg((g * 4 + t4) * BLK, (g * 4 + t4 + 1) * BLK)], dkey("oo"))
        p.finalize_and_emit()
    return nc


def fm(v):
    v = np.asarray(v, np.float32)
    return np.ascontiguousarray(v.reshape(-1, 128).T)


def make_inputs(inputs):
    f = lambda k: np.asarray(inputs[k], np.float32)
    x, c = f("x"), f("c")
    cols = [fm(f("norm_mix_g").reshape(-1)), fm(f("norm_mlp_g").reshape(-1)), fm(f("final_norm_g")),
            fm(f("conv_b_pw1").reshape(-1)), fm(f("conv_b_dw").reshape(-1)), fm(f("conv_ln_g").reshape(-1)),
            fm(f("conv_ln_b").reshape(-1)), fm(f("conv_b_pw2").reshape(-1))]
    wdw = f("conv_w_dw")
    wdw = wdw.reshape(2, 31, 8, 128).transpose(3, 0, 2, 1).reshape(128, 2 * 8 * 31)
    vec = np.ascontiguousarray(np.concatenate(cols + [wdw], axis=1).astype(np.float32))
    assert vec.shape == (128, NV)
    gn = np.stack([f("ret_gn_g"), f("ret_gn_b")], 0)
    gnrep = np.ascontiguousarray(np.broadcast_to(gn[:, :, :, None, :], (2, 2, 4, 128, 512)).reshape(-1, 512))
    shared = {
        "vec": vec, "gnrep": gnrep,
        "ada_w": f("ada_w").reshape(DEPTH * D, 6 * D), "ada_b": f("ada_b").reshape(1, -1),
        "conv_w_pw1": f("conv_w_pw1").reshape(2 * D, 2 * D), "conv_w_pw2": f("conv_w_pw2").reshape(2 * D, D),
        "ret_w_in": f("ret_w_in").reshape(2 * D, 6 * D), "ret_w_out": f("ret_w_out").reshape(4 * D, D),
        "mlp_w1": f("mlp_w1").reshape(DEPTH * D, 4 * D), "mlp_w2": f("mlp_w2").reshape(DEPTH * 4 * D, D),
    }
    maps = []
    for r in range(8):
        b, half = r // 2, r % 2
        pos = (half * NT + np.arange(NT, dtype=np.float32)).reshape(32, 128).T
        m = dict(shared)
        m["x_sh"] = np.ascontiguousarray(x[b, half * NT:(half + 1) * NT, :])
        m["cvec"] = fm(c[b])
        m["flag"] = np.full((128, 1), float(half), np.float32)
        m["pos"] = np.ascontiguousarray(pos.astype(np.float32))
        maps.append(m)
    return maps


_NC_CACHE = {}


def kernel(**inputs):
    maps = make_inputs(inputs)
    if "nc" not in _NC_CACHE:
        _NC_CACHE["nc"] = build()
    res = run_bass_kernel_spmd(_NC_CACHE["nc"], maps, core_ids=list(range(8)))
    outp = np.empty((4, 2 * NT, D), np.float32)
    for r in range(8):
        outp[r // 2, (r % 2) * NT:(r % 2 + 1) * NT, :] = res.results[r]["out"]
    return outp
```

```python
import contextlib
import math
import numpy as np
import concourse.bass as bass
import concourse.mybir as mybir
from concourse.bass_utils import run_bass_kernel_spmd

F32 = mybir.dt.float32
BF16 = mybir.dt.bfloat16
I32 = mybir.dt.int32
ALU = mybir.AluOpType
AF = mybir.ActivationFunctionType

BLK = 64


class Buf:
    def __init__(self, name, t, nelem):
        self.name = name
        self.t = t
        self.n = (nelem + BLK - 1) // BLK
        self.lastw = [None] * self.n
        self.readers = [[] for _ in range(self.n)]

    def rg(self, lo=0, hi=None):
        if hi is None:
            hi = self.n * BLK
        return (self, lo // BLK, (hi + BLK - 1) // BLK)


class Op:
    __slots__ = ("eng", "fn", "deps", "is_async", "key", "inc", "signal", "sigval", "waits", "idx")

    def __init__(self, eng, fn, is_async, key, inc):
        self.eng = eng
        self.fn = fn
        self.deps = set()
        self.is_async = is_async
        self.key = key
        self.inc = inc
        self.signal = is_async
        self.sigval = None
        self.waits = []


class Prog:
    ENGS = ("pe", "act", "dve", "pool", "sp")

    def __init__(self, nc, same_engine_sync=True):
        self.nc = nc
        self.ops = []
        self.same = same_engine_sync
        self.async_counts = {}

    def add(self, eng, fn, reads=(), writes=(), async_key=None, inc=16):
        op = Op(eng, fn, async_key is not None, async_key, inc)
        op.idx = len(self.ops)
        for (b, lo, hi) in reads:
            for i in range(lo, hi):
                w = b.lastw[i]
                if w is not None:
                    op.deps.add(w)
        for (b, lo, hi) in writes:
            for i in range(lo, hi):
                w = b.lastw[i]
                if w is not None:
                    op.deps.add(w)
                for r in b.readers[i]:
                    op.deps.add(r)
        for (b, lo, hi) in reads:
            for i in range(lo, hi):
                b.readers[i].append(op)
        for (b, lo, hi) in writes:
            for i in range(lo, hi):
                b.lastw[i] = op
                b.readers[i] = []
        op.deps.discard(op)
        if op.is_async:
            c = self.async_counts.get(async_key, 0) + inc
            self.async_counts[async_key] = c
            op.sigval = c
        self.ops.append(op)
        return op

    def _needs_wait(self, op, d):
        if d.is_async:
            return True
        if d.eng != op.eng:
            return True
        if op.is_async:
            return True
        if op.eng == "pe":
            return False
        return self.same

    def finalize_and_emit(self):
        nc = self.nc
        for op in self.ops:
            op.deps = [d for d in op.deps if self._needs_wait(op, d)]
            for d in op.deps:
                d.signal = True
        cnt = {e: 0 for e in self.ENGS}
        for op in self.ops:
            if not op.is_async and op.signal:
                cnt[op.eng] += 1
                op.sigval = cnt[op.eng]
        waited = {e: {} for e in self.ENGS}
        for op in self.ops:
            need = {}
            for d in op.deps:
                k = ("A", d.key) if d.is_async else ("E", d.eng)
                if d.sigval > need.get(k, 0):
                    need[k] = d.sigval
            w = waited[op.eng]
            for k, v in need.items():
                if w.get(k, 0) < v:
                    w[k] = v
                    op.waits.append((k, v))
        with contextlib.ExitStack() as es:
            sems = {}
            for e in ("pe", "act", "dve", "pool"):
                sems[("E", e)] = es.enter_context(nc.semaphore("s_" + e))
            for k in self.async_counts:
                sems[("A", k)] = es.enter_context(nc.semaphore("a_" + str(k)))
            block = es.enter_context(nc.Block())
            per = {e: [op for op in self.ops if op.eng == e] for e in self.ENGS}

            def run(engobj, lst, ename):
                for op in lst:
                    for (k, v) in op.waits:
                        engobj.wait_ge(sems[k], v)
                    ins = op.fn(engobj)
                    if op.is_async:
                        if op.inc == 16:
                            ins.then_inc(sems[("A", op.key)], 16)
                        else:
                            ins.then_inc(sems[("A", op.key)])
                    elif op.signal:
                        ins.then_inc(sems[("E", ename)], 1)
                last = {}
                for op in lst:
                    if op.is_async:
                        last[op.key] = max(last.get(op.key, 0), op.sigval)
                for k, v in last.items():
                    engobj.wait_ge(sems[("A", k)], v)

            @block.tensor
            def _(e):
                run(e, per["pe"], "pe")

            @block.scalar
            def _(e):
                run(e, per["act"], "act")

            @block.vector
            def _(e):
                run(e, per["dve"], "dve")

            @block.gpsimd
            def _(e):
                run(e, per["pool"], "pool")

            @block.sync
            def _(e):
                run(e, per["sp"], "sp")


PIPE_SKEW = 1
D = 1024
NT = 4096
T = 512
NG = NT // T
DEPTH = 4
EPS = 1e-6
HEADS = 4
LG = [math.log(1.0 - 2.0 ** (-5.0 - h)) for h in range(HEADS)]
LN16 = math.log(1.0 / 16.0)

VC_NMIX = 0
VC_NMLP = 32
VC_FIN = 64
VC_BPW1 = 72
VC_BDW = 104
VC_LNG = 120
VC_LNB = 136
VC_BPW2 = 152
VC_WDW = 168
NV = 168 + 2 * 8 * 31


def build(stop_after=None, debug_dump=False):
    nc = bass.Bass("TRN2", target_bir_lowering=False)

    def din(name, shape, dt=F32):
        return nc.dram_tensor(name, shape, dt, kind="ExternalInput").ap()

    x_in = din("x_sh", [NT, D])
    cvec = din("cvec", [128, 8])
    flag_in = din("flag", [128, 1])
    pos_in = din("pos", [128, 32])
    vec_in = din("vec", [128, NV])
    gn_in = din("gnrep", [2 * 2 * 4 * 128, 512])
    ada_w = din("ada_w", [DEPTH * D, 6 * D])
    ada_b = din("ada_b", [1, DEPTH * 6 * D])
    w_pw1 = din("conv_w_pw1", [2 * D, 2 * D])
    w_pw2 = din("conv_w_pw2", [2 * D, D])
    w_rin = din("ret_w_in", [2 * D, 6 * D])
    w_rout = din("ret_w_out", [2 * 2 * D, D])
    w_m1 = din("mlp_w1", [DEPTH * D, 4 * D])
    w_m2 = din("mlp_w2", [DEPTH * 4 * D, D])
    out = nc.dram_tensor("out", [NT, D], F32, kind="ExternalOutput").ap()
    dbg = nc.dram_tensor("dbg", [8 * D, 1024], F32, kind="ExternalOutput").ap() if debug_dump else None

    xs_t = nc.dram_tensor("xs", [D, NT], F32)
    hs_t = nc.dram_tensor("hs", [D, NT], BF16)
    tab_t = nc.dram_tensor("tabs", [32 * 128, 1024], F32)
    kv_t = nc.dram_tensor("kvs", [32 * 128, 3072], BF16)
    hb_t = [nc.dram_tensor("halo_b%d" % i, [1024, 32], F32) for i in range(2)]
    hg_t = [nc.dram_tensor("halo_g%d" % i, [2048, 32], F32) for i in range(2)]
    sb_t = [nc.dram_tensor("st_b%d" % i, [1024, 512], F32) for i in range(2)]
    sg_t = [nc.dram_tensor("st_g%d" % i, [2048, 512], F32) for i in range(2)]
    xs, hs, tabs = xs_t.ap(), hs_t.ap(), tab_t.ap()
    xs_v = xs.rearrange("(k p) t -> p k t", p=128)
    hs_v = hs.rearrange("(k p) t -> p k t", p=128)
    PAIRS = [[0, 1], [2, 3], [4, 5], [6, 7]]

    with contextlib.ExitStack() as es:
        def sb(name, n, dt):
            t = es.enter_context(nc.sbuf_tensor(name, [128, n], dt))
            return Buf(name, t, n)

        def psb(name, n, dt):
            t = es.enter_context(nc.psum_tensor(name, [128, n], dt))
            return Buf(name, t, n)

        WA = sb("WA", 32768, BF16)
        WB = sb("WB", 32768, BF16)
        XT = sb("XT", 4096, F32)
        H = sb("H", 4096, BF16)
        BFA = sb("BFA", 8192, BF16)
        FA = sb("FA", 4096, F32)
        FB = sb("FB", 3072, F32)
        VEC = sb("VEC", NV, F32)
        MODV = sb("MODV", DEPTH * 48, F32)
        IDF = sb("IDF", 128, F32)
        IDB = sb("IDB", 128, BF16)
        ONES = sb("ONES", 128, BF16)
        CST = sb("CST", 16, F32)
        MASK = sb("MASK", 512, F32)
        XI = sb("XI", 1024, F32)
        SM = sb("SM", 64, F32)
        CB = sb("CB", 8, BF16)
        II = sb("II", 256, I32)
        PS = [psb("PS%d" % i, 512, F32) for i in range(6)]
        PB = [psb("PB%d" % i, 1024, BF16) for i in range(2)]
        XSB = Buf("xs", xs_t, NG * BLK)
        HSB = Buf("hs", hs_t, NG * BLK)
        TABB = Buf("tabs", tab_t, 32 * BLK)
        KVB = Buf("kvs", kv_t, 32 * BLK)
        HBB = [Buf("hb%d" % i, hb_t[i], BLK) for i in range(2)]
        HGB = [Buf("hg%d" % i, hg_t[i], BLK) for i in range(2)]
        SBB = [Buf("sbb%d" % i, sb_t[i], BLK) for i in range(2)]
        SGB = [Buf("sgb%d" % i, sg_t[i], BLK) for i in range(2)]
        OUTB = Buf("out", None, 32 * BLK)

        p = Prog(nc)
        st = {"ps": 0, "fb": 0, "dk": 0}

        def next_ps():
            b = PS[st["ps"] % 6]
            st["ps"] += 1
            return b

        def dkey(prefix):
            st["dk"] += 1
            return "%s%d" % (prefix, st["dk"] % 4)

        def mm(ps_buf, plo, n, lhsT, lrg, rhs, rrg, start, stop, m=128):
            o = ps_buf.t[0:m, plo:plo + n]
            p.add("pe", lambda e: e.matmul(o, lhsT=lhsT, rhs=rhs, start=start, stop=stop),
                  reads=[lrg, rrg], writes=[ps_buf.rg(plo, plo + n)])

        def act(out_ap, in_ap, func, reads, writes, bias=None, scale=None):
            kw = {}
            if bias is not None:
                kw["bias"] = bias
            if scale is not None:
                kw["scale"] = scale
            p.add("act", lambda e: e.activation(out=out_ap, in_=in_ap, func=func, **kw), reads=reads, writes=writes)

        def tt(eng, out_ap, a, b, op, reads, writes):
            p.add(eng, lambda e: e.tensor_tensor(out=out_ap, in0=a, in1=b, op=op), reads=reads, writes=writes)

        def ts(eng, out_ap, a, s1, op0, reads, writes, s2=None, op1=None):
            if op1 is None:
                p.add(eng, lambda e: e.tensor_scalar(out=out_ap, in0=a, scalar1=s1, scalar2=None, op0=op0),
                      reads=reads, writes=writes)
            else:
                p.add(eng, lambda e: e.tensor_scalar(out=out_ap, in0=a, scalar1=s1, scalar2=s2, op0=op0, op1=op1),
                      reads=reads, writes=writes)

        def stt(out_ap, a, s, b, op0, op1, reads, writes):
            p.add("dve", lambda e: e.scalar_tensor_tensor(out=out_ap, in0=a, scalar=s, in1=b, op0=op0, op1=op1),
                  reads=reads, writes=writes)

        def cp(eng, out_ap, in_ap, reads, writes):
            if eng == "act":
                act(out_ap, in_ap, AF.Copy, reads, writes)
            else:
                p.add(eng, lambda e: e.tensor_copy(out=out_ap, in_=in_ap), reads=reads, writes=writes)

        def dma(eng, out_ap, in_ap, reads, writes, key):
            p.add(eng, lambda e: e.dma_start(out=out_ap, in_=in_ap), reads=reads, writes=writes, async_key=key)

        def fb_slot():
            s = st["fb"] % 6
            st["fb"] += 1
            return s * 512

        def recip(buf, lo, n):
            ap_ = buf.t[:, lo:lo + n]
            p.add("dve", lambda e: e.reciprocal(out=ap_, in_=ap_), reads=[buf.rg(lo, lo + n)], writes=[buf.rg(lo, lo + n)])

        def bnstats(pbuf):
            src_ = pbuf.t[:]
            p.add("dve", lambda e: e.bn_stats(out=SM.t[:, 32:38], in_=src_), reads=[pbuf.rg()], writes=[SM.rg(32, 38)])
            p.add("dve", lambda e: e.bn_aggr(out=SM.t[:, 40:42], in_=SM.t[:, 32:38]), reads=[SM.rg(32, 38)], writes=[SM.rg(40, 42)])

        EPS_AP = CST.t[:, 0:1]
        FLAG_AP = CST.t[:, 1:2]

        dma("sp", VEC.t[:], vec_in, [], [VEC.rg()], "c0")
        dma("sp", CST.t[:, 1:2], flag_in, [], [CST.rg()], "c1")
        p.add("dve", lambda e: e.memset(CST.t[:, 0:1], EPS), writes=[CST.rg()])
        p.add("dve", lambda e: e.memset(CST.t[:, 6:7], LN16), writes=[CST.rg()])
        p.add("dve", lambda e: e.memset(ONES.t[:], 1.0 / 1024.0), writes=[ONES.rg()])
        p.add("pool", lambda e: e.iota(II.t[:, 0:128], pattern=[[1, 128]], base=0, channel_multiplier=-1), writes=[II.rg()])
        cp("dve", FB.t[:, 0:128], II.t[:, 0:128], [II.rg()], [FB.rg(0, 128)])
        ts("dve", IDF.t[:], FB.t[:, 0:128], 0.0, ALU.is_equal, [FB.rg(0, 128)], [IDF.rg()])
        cp("dve", IDB.t[:], IDF.t[:], [IDF.rg()], [IDB.rg()])
        ts("dve", FB.t[:, 128:256], FB.t[:, 0:128], -1.0, ALU.mult, [FB.rg(0, 128)], [FB.rg(128, 256)])
        tt("dve", FB.t[:, 0:128], FB.t[:, 0:128], FB.t[:, 128:256], ALU.max, [FB.rg(0, 256)], [FB.rg(0, 128)])
        for h in range(HEADS):
            act(MASK.t[:, h * 128:(h + 1) * 128], FB.t[:, 0:128], AF.Exp, [FB.rg(0, 128), CST.rg()], [MASK.rg(h * 128, (h + 1) * 128)],
                bias=CST.t[:, 6:7], scale=LG[h])
        p.add("dve", lambda e: e.memset(MASK.t[64:128, :].rearrange("p (h c) -> p h c", h=4)[:, :, 0:64], 0.0),
              reads=[MASK.rg()], writes=[MASK.rg()])
        p.add("pool", lambda e: e.iota(II.t[:, 0:128], pattern=[[1, 128]], base=1, channel_multiplier=0), reads=[II.rg()], writes=[II.rg()])
        cp("dve", FB.t[:, 256:384], II.t[:, 0:128], [II.rg()], [FB.rg(256, 384)])
        for h in range(HEADS):
            for r in range(2):
                o = h * 256 + r * 128
                act(XI.t[:, o:o + 128], FB.t[:, 256:384], AF.Exp, [FB.rg(256, 384)], [XI.rg(o, o + 128)], scale=LG[h])
        p.add("pool", lambda e: e.iota(II.t[:, 0:1], pattern=[[0, 1]], base=127, channel_multiplier=-1), reads=[II.rg()], writes=[II.rg()])
        cp("dve", FB.t[:, 384:385], II.t[:, 0:1], [II.rg()], [FB.rg(384, 385)])
        for h in range(HEADS):
            act(CST.t[:, 2 + h:3 + h], FB.t[:, 384:385], AF.Exp, [FB.rg(384, 385), CST.rg()], [CST.rg()],
                bias=CST.t[:, 6:7], scale=LG[h])
        DEC = [math.exp(LG[h] * 128.0) for h in range(HEADS)]

        dma("sp", SM.t[:, 48:56], cvec, [], [SM.rg(48, 56)], "c3")
        act(CB.t[:], SM.t[:, 48:56], AF.Silu, [SM.rg(48, 56)], [CB.rg()])
        ROW = XT
        def ada_block(i, half):
            dma("sp", FB.t[0:1, 0:3072], ada_b[0:1, i * 6144 + half * 3072:i * 6144 + (half + 1) * 3072], [],
                [FB.rg(0, 3072)], dkey("ab"))
            for ct in range(6):
                col = (half * 6 + ct) * 512
                wb = WA if ct % 2 == 0 else WB
                wo = (ct // 2) * 4096
                src = ada_w[i * D:(i + 1) * D, col:col + 512].rearrange("(k p) n -> p k n", p=128)
                dst = wb.t[:, wo:wo + 4096].rearrange("p (k n) -> p k n", k=8)
                dma("pool", dst, src, [], [wb.rg(wo, wo + 4096)], dkey("aw"))
                pb = PS[ct]
                for k in range(8):
                    mm(pb, 0, 512, CB.t[:, k:k + 1], CB.rg(), wb.t[:, wo + k * 512:wo + (k + 1) * 512],
                       wb.rg(wo + k * 512, wo + (k + 1) * 512), k == 0, False, m=1)
                mm(pb, 0, 512, IDF.t[0:1, 0:1], IDF.rg(), FB.t[0:1, ct * 512:(ct + 1) * 512], FB.rg(ct * 512, (ct + 1) * 512),
                   False, True, m=1)
                cp("act", XT.t[0:1, ct * 512:(ct + 1) * 512], pb.t[0:1, :], [pb.rg()], [XT.rg(ct * 512, (ct + 1) * 512)])
            pb = PS[0]
            for j in range(24):
                o = pb.t[:, j:j + 1]
                lhsT = XT.t[0:1, j * 128:(j + 1) * 128]
                rhs = IDF.t[0:1, 0:1]
                p.add("pe", (lambda o, lhsT, rhs: lambda e: e.matmul(o, lhsT=lhsT, rhs=rhs, start=True, stop=True))(o, lhsT, rhs),
                      reads=[XT.rg(j * 128, (j + 1) * 128), IDF.rg()], writes=[pb.rg(0, 24)])
            cp("act", MODV.t[:, i * 48 + half * 24:i * 48 + (half + 1) * 24], pb.t[:, 0:24], [pb.rg(0, 24)],
               [MODV.rg(i * 48 + half * 24, i * 48 + (half + 1) * 24)])

        POS = FA
        dma("sp", FA.t[:, 0:32], pos_in, [], [FA.rg(0, 32)], "c2")
        p.add("dve", lambda e: e.memset(FA.t[:, 128:129], 1.0), writes=[FA.rg(128, 129)])
        rr = 10000.0 ** (-2.0 / 256.0)
        for s in range(7):
            n = 1 << s
            ts("dve", FA.t[:, 128 + n:128 + 2 * n], FA.t[:, 128:128 + n], float(np.float32(rr ** n)), ALU.mult,
               [FA.rg(128, 256)], [FA.rg(128, 256)])
        TWO_PI = 2.0 * math.pi
        def rope_tile(tI):
            a = 512 + (tI % 2) * 1792
            ANG = FA.t[:, a:a + 256]
            NN = FA.t[:, a + 256:a + 512]
            TAB = a + 768
            rga = FA.rg(a, a + 1792)
            ts("dve", FA.t[:, a + 128:a + 256], FA.t[:, 128:256], FA.t[:, tI:tI + 1], ALU.mult, [FA.rg(0, 256)], [rga])
            ts("dve", FA.t[:, a:a + 128], FA.t[:, a + 128:a + 256], math.pi / 2.0, ALU.add, [rga], [rga])
            ts("dve", NN, ANG, 1.0 / TWO_PI, ALU.mult, [rga], [rga])
            cp("dve", II.t[:], NN, [rga], [II.rg()])
            cp("dve", NN, II.t[:], [II.rg()], [rga])
            stt(ANG, NN, -6.28125, ANG, ALU.mult, ALU.add, [rga], [rga])
            stt(ANG, NN, -0.0019353071795864769, ANG, ALU.mult, ALU.add, [rga], [rga])
            ts("dve", NN, ANG, math.pi, ALU.is_gt, [rga], [rga], s2=TWO_PI, op1=ALU.mult)
            tt("dve", ANG, ANG, NN, ALU.subtract, [rga], [rga])
            ts("dve", ANG, ANG, math.pi, ALU.min, [rga], [rga], s2=-math.pi, op1=ALU.max)
            act(FA.t[:, a + 512:a + 768], ANG, AF.Sin, [rga], [rga])
            COS = FA.t[:, a + 512:a + 640]
            SIN = FA.t[:, a + 640:a + 768]
            for r in range(4):
                cp("act" if r % 2 else "pool", FA.t[:, TAB + r * 128:TAB + (r + 1) * 128], COS, [rga], [rga])
            for r in range(4):
                o = TAB + 512 + r * 128
                if r % 2 == 0:
                    ts("dve", FA.t[:, o:o + 128], SIN, -1.0, ALU.mult, [rga], [rga])
                else:
                    cp("pool", FA.t[:, o:o + 128], SIN, [rga], [rga])
            dma("sp", tabs[tI * 128:(tI + 1) * 128, :], FA.t[:, TAB:TAB + 1024], [rga], [TABB.rg(tI * BLK, (tI + 1) * BLK)], dkey("tb"))

        for blk in range(8):
            ada_block(blk // 2, blk % 2)
            for q4 in range(4):
                rope_tile(blk * 4 + q4)

        def modcol(i, j):
            o = i * 48 + j * 8
            return MODV.t[:, o:o + 8], MODV.rg(o, o + 8)

        def load_xin(g):
            src = x_in[g * T:(g + 1) * T, :].rearrange("(t p) d -> p t d", p=128)
            dma("sp", FA.t[:].rearrange("p (t d) -> p t d", t=4), src, [], [FA.rg()], dkey("xi"))

        for g in range(NG):
            if g == 0:
                load_xin(0)
            for c in range(8):
                pb = next_ps()
                for t4 in range(4):
                    o = pb.t[:, t4 * 128:(t4 + 1) * 128]
                    i_ = FA.t[:, t4 * 1024 + c * 128:t4 * 1024 + (c + 1) * 128]
                    p.add("pe", (lambda o, i_: lambda e: e.transpose(o, i_, IDF.t[:]))(o, i_),
                          reads=[FA.rg(t4 * 1024 + c * 128, t4 * 1024 + (c + 1) * 128), IDF.rg()],
                          writes=[pb.rg(t4 * 128, (t4 + 1) * 128)])
                cp("act" if c % 2 else "dve", XT.t[:, c * 512:(c + 1) * 512], pb.t[:], [pb.rg()], [XT.rg(c * 512, (c + 1) * 512)])
            if g + 1 < NG:
                load_xin(g + 1)
            dma("sp", xs_v[:, :, g * T:(g + 1) * T], XT.t[:].rearrange("p (k t) -> p k t", k=8), [XT.rg()], [XSB.rg(g * BLK, (g + 1) * BLK)], dkey("xo"))

        def load_x(g):
            dma("sp", XT.t[:].rearrange("p (k t) -> p k t", k=8), xs_v[:, :, g * T:(g + 1) * T],
                [XSB.rg(g * BLK, (g + 1) * BLK)], [XT.rg()], dkey("xl"))

        def load_x_chunk(g, c):
            dma("sp", XT.t[:, c * T:(c + 1) * T], xs[c * 128:(c + 1) * 128, g * T:(g + 1) * T],
                [XSB.rg(g * BLK, (g + 1) * BLK)], [XT.rg(c * T, (c + 1) * T)], dkey("xl"))

        def store_x_chunk(g, c):
            dma("sp", xs[c * 128:(c + 1) * 128, g * T:(g + 1) * T], XT.t[:, c * T:(c + 1) * T],
                [XT.rg(c * T, (c + 1) * T)], [XSB.rg(g * BLK, (g + 1) * BLK)], dkey("xo"))

        def store_x(g, eng="sp"):
            dma(eng, xs_v[:, :, g * T:(g + 1) * T], XT.t[:].rearrange("p (k t) -> p k t", k=8),
                [XT.rg()], [XSB.rg(g * BLK, (g + 1) * BLK)], dkey("xo"))

        def prep_mod(i, which):
            gcol = (VC_NMIX if which == 0 else VC_NMLP) + i * 8
            sh, shr = modcol(i, 3 * which + 0)
            sc, scr = modcol(i, 3 * which + 1)
            gt, gtr = modcol(i, 3 * which + 2)
            stt(SM.t[:, 0:8], sc, 1.0, VEC.t[:, gcol:gcol + 8], ALU.add, ALU.mult, [scr, VEC.rg()], [SM.rg(0, 8)])
            cp("dve", SM.t[:, 8:16], sh, [shr], [SM.rg(8, 16)])
            cp("dve", SM.t[:, 16:24], gt, [gtr], [SM.rg(16, 24)])

        def norm_mod(src_buf, sofs, n, dst_buf, dofs, sq_buf, sq_ofs, a_ofs=0, b_ofs=8, with_b=True):
            pb = next_ps()
            for c in range(8):
                so = sq_ofs + (c % 2) * n
                act(sq_buf.t[:, so:so + n], src_buf.t[:, sofs + c * n:sofs + (c + 1) * n], AF.Square,
                    [src_buf.rg(sofs + c * n, sofs + (c + 1) * n)], [sq_buf.rg(so, so + n)])
                mm(pb, 0, n, ONES.t[:], ONES.rg(), sq_buf.t[:, so:so + n], sq_buf.rg(so, so + n), c == 0, c == 7)
            r0 = fb_slot()
            act(FB.t[:, r0:r0 + n], pb.t[:, 0:n], AF.Sqrt, [pb.rg(0, n), CST.rg()], [FB.rg(r0, r0 + n)], bias=EPS_AP, scale=1.0)
            recip(FB, r0, n)
            for c in range(8):
                t0 = fb_slot()
                while t0 == r0:
                    t0 = fb_slot()
                stt(FB.t[:, t0:t0 + n], src_buf.t[:, sofs + c * n:sofs + (c + 1) * n], SM.t[:, a_ofs + c:a_ofs + c + 1],
                    FB.t[:, r0:r0 + n], ALU.mult, ALU.mult,
                    [src_buf.rg(sofs + c * n, sofs + (c + 1) * n), SM.rg(), FB.rg(r0, r0 + n)], [FB.rg(t0, t0 + n)])
                if with_b:
                    act(dst_buf.t[:, dofs + c * n:dofs + (c + 1) * n], FB.t[:, t0:t0 + n], AF.Identity,
                        [FB.rg(t0, t0 + n), SM.rg()], [dst_buf.rg(dofs + c * n, dofs + (c + 1) * n)],
                        bias=SM.t[:, b_ofs + c:b_ofs + c + 1], scale=1.0)
                else:
                    cp("act", dst_buf.t[:, dofs + c * n:dofs + (c + 1) * n], FB.t[:, t0:t0 + n],
                       [FB.rg(t0, t0 + n)], [dst_buf.rg(dofs + c * n, dofs + (c + 1) * n)])

        def load_w(dst_buf, dofs, src_ap_2d, nk, ncols, key):
            src = src_ap_2d.rearrange("(k p) n -> p k n", p=128)
            for k0 in range(0, nk, 2):
                k1 = min(nk, k0 + 2)
                d = dst_buf.t[:, dofs + k0 * ncols:dofs + k1 * ncols].rearrange("p (k n) -> p k n", k=k1 - k0)
                dma("pool", d, src[:, k0:k1, :], [], [dst_buf.rg(dofs + k0 * ncols, dofs + k1 * ncols)], key)

        def load_w_strided(dst_buf, dofs, dstride, src_ap_2d, nk, ncols, key):
            src = src_ap_2d.rearrange("(k p) n -> p k n", p=128)
            d = dst_buf.t[:, dofs:dofs + nk * dstride].rearrange("p (k n) -> p k n", k=nk)[:, :, 0:ncols]
            dma("pool", d, src, [], [dst_buf.rg(dofs, dofs + nk * dstride)], key)

        def conv_layer(i):
            l = i // 2
            GLW = 544
            prep_mod(i, 0)
            tt("dve", SM.t[:, 24:32], VEC.t[:, VC_BPW2 + l * 8:VC_BPW2 + l * 8 + 8], SM.t[:, 16:24], ALU.mult,
               [VEC.rg(), SM.rg(16, 24)], [SM.rg(24, 32)])
            load_w(WA, 0, w_pw1[l * D:(l + 1) * D, :], 8, 2048, "wa")
            load_w(WA, 16384, w_pw2[l * D:(l + 1) * D, :], 8, 1024, "wa")
            for c in range(8):
                for k in range(31):
                    o = (c * 31 + k) * 128
                    col = VC_WDW + (l * 8 + c) * 31 + k
                    if (c * 31 + k) % 2:
                        ts("dve", WB.t[:, o:o + 128], IDF.t[:], VEC.t[:, col:col + 1], ALU.mult,
                           [IDF.rg(), VEC.rg()], [WB.rg(o, o + 128)])
                    else:
                        act(WB.t[:, o:o + 128], IDF.t[:], AF.Identity, [IDF.rg(), VEC.rg()], [WB.rg(o, o + 128)],
                            scale=VEC.t[:, col:col + 1])

            def glu_for(hbuf, hofs, n, emit_out):
                for c in range(8):
                    pa, pg = next_ps(), next_ps()
                    for k in range(8):
                        mm(pa, 0, n, WA.t[:, k * 2048 + c * 128:k * 2048 + c * 128 + 128], WA.rg(k * 2048 + c * 128, k * 2048 + c * 128 + 128),
                           hbuf.t[:, hofs + k * n:hofs + (k + 1) * n], hbuf.rg(hofs + k * n, hofs + (k + 1) * n), k == 0, k == 7)
                    for k in range(8):
                        o = k * 2048 + 1024 + c * 128
                        mm(pg, 0, n, WA.t[:, o:o + 128], WA.rg(o, o + 128),
                           hbuf.t[:, hofs + k * n:hofs + (k + 1) * n], hbuf.rg(hofs + k * n, hofs + (k + 1) * n), k == 0, k == 7)
                    s0 = fb_slot()
                    bcol = VC_BPW1 + l * 16
                    act(FB.t[:, s0:s0 + n], pg.t[:, 0:n], AF.Sigmoid, [pg.rg(0, n), VEC.rg()], [FB.rg(s0, s0 + n)],
                        bias=VEC.t[:, bcol + 8 + c:bcol + 9 + c], scale=1.0)
                    emit_out(c, pa, s0, VEC.t[:, bcol + c:bcol + c + 1])

            dma("sp", FA.t[:, 0:256].rearrange("p (k t) -> p k t", k=8), xs_v[:, :, NT - 32:NT], [XSB.rg((NG - 1) * BLK, NG * BLK)],
                [FA.rg(0, 256)], dkey("hl"))
            norm_mod(FA, 0, 32, BFA, 7168, BFA, 7680)
            def halo_out(c, pa, s0, bap):
                stt(FA.t[:, 256 + c * 32:256 + (c + 1) * 32], pa.t[:, 0:32], bap, FB.t[:, s0:s0 + 32], ALU.add, ALU.mult,
                    [pa.rg(0, 32), VEC.rg(), FB.rg(s0, s0 + 32)], [FA.rg(256 + c * 32, 256 + (c + 1) * 32)])
            glu_for(BFA, 7168, 32, halo_out)
            xi_ = l
            dma("pool", hb_t[xi_].ap().rearrange("(k p) t -> p k t", p=128), FA.t[:, 256:512].rearrange("p (k t) -> p k t", k=8),
                [FA.rg(256, 512)], [HBB[xi_].rg()], "hx")
            p.add("pool", lambda e: e.collective_compute("AllGather", ALU.bypass, replica_groups=PAIRS,
                                                         ins=[hb_t[xi_].ap().opt()], outs=[hg_t[xi_].ap().opt()]),
                  reads=[HBB[xi_].rg()], writes=[HGB[xi_].rg()], async_key="cc%d" % i, inc=1)
            dma("pool", FA.t[:, 512:768].rearrange("p (k t) -> p k t", k=8), hg_t[xi_].ap()[0:1024, :].rearrange("(k p) t -> p k t", p=128),
                [HGB[xi_].rg()], [FA.rg(512, 768)], "hx")
            ts("dve", BFA.t[:, 0:8 * GLW].rearrange("p (k t) -> p k t", k=8)[:, :, 0:32],
               FA.t[:, 512:768].rearrange("p (k t) -> p k t", k=8), FLAG_AP, ALU.mult,
               [FA.rg(512, 768), CST.rg()], [BFA.rg(0, 8 * GLW)])

            for g in range(NG):
                if g == 0:
                    load_x(g)
                norm_mod(XT, 0, T, H, 0, BFA, 4352)
                def glu_out(c, pa, s0, bap):
                    o = c * GLW + 32
                    stt(BFA.t[:, o:o + T], pa.t[:], bap, FB.t[:, s0:s0 + T], ALU.add, ALU.mult,
                        [pa.rg(), VEC.rg(), FB.rg(s0, s0 + T)], [BFA.rg(o, o + T)])
                glu_for(H, 0, T, glu_out)
                pmean, pmsq = PS[4], PS[5]
                for c in range(8):
                    pc = PS[c % 4]
                    for k in range(31):
                        o = (c * 31 + k) * 128
                        ro = c * GLW + 2 + k
                        mm(pc, 0, T, WB.t[:, o:o + 128], WB.rg(o, o + 128), BFA.t[:, ro:ro + T], BFA.rg(ro, ro + T), k == 0, k == 30)
                    bdw = VEC.t[:, VC_BDW + l * 8 + c:VC_BDW + l * 8 + c + 1]
                    act(FA.t[:, c * T:(c + 1) * T], pc.t[:], AF.Identity, [pc.rg(), VEC.rg()], [FA.rg(c * T, (c + 1) * T)], bias=bdw, scale=1.0)
                    so = 4352 + (c % 2) * T
                    act(BFA.t[:, so:so + T], pc.t[:], AF.Square, [pc.rg(), VEC.rg()], [BFA.rg(so, so + T)], bias=bdw, scale=1.0)
                    uo = 5376 + (c % 2) * T
                    cp("dve", BFA.t[:, uo:uo + T], FA.t[:, c * T:(c + 1) * T], [FA.rg(c * T, (c + 1) * T)], [BFA.rg(uo, uo + T)])
                    mm(pmean, 0, T, ONES.t[:], ONES.rg(), BFA.t[:, uo:uo + T], BFA.rg(uo, uo + T), c == 0, c == 7)
                    mm(pmsq, 0, T, ONES.t[:], ONES.rg(), BFA.t[:, so:so + T], BFA.rg(so, so + T), c == 0, c == 7)
                gv = BFA.t[:, 0:8 * GLW].rearrange("p (k t) -> p k t", k=8)
                cp("pool", gv[:, :, 0:32], gv[:, :, T:T + 32], [BFA.rg(0, 8 * GLW)], [BFA.rg(0, 8 * GLW)])
                mu, rs = fb_slot(), fb_slot()
                cp("act", FB.t[:, mu:mu + T], pmean.t[:], [pmean.rg()], [FB.rg(mu, mu + T)])
                tt("dve", FB.t[:, rs:rs + T], FB.t[:, mu:mu + T], FB.t[:, mu:mu + T], ALU.mult, [FB.rg(mu, mu + T)], [FB.rg(rs, rs + T)])
                tt("dve", FB.t[:, rs:rs + T], pmsq.t[:], FB.t[:, rs:rs + T], ALU.subtract, [pmsq.rg(), FB.rg(rs, rs + T)], [FB.rg(rs, rs + T)])
                ts("dve", FB.t[:, rs:rs + T], FB.t[:, rs:rs + T], 0.0, ALU.max, [FB.rg(rs, rs + T)], [FB.rg(rs, rs + T)])
                act(FB.t[:, rs:rs + T], FB.t[:, rs:rs + T], AF.Sqrt, [FB.rg(rs, rs + T), CST.rg()], [FB.rg(rs, rs + T)], bias=EPS_AP, scale=1.0)
                recip(FB, rs, T)
                for c in range(8):
                    d0 = fb_slot()
                    while d0 in (mu, rs):
                        d0 = fb_slot()
                    tt("dve", FB.t[:, d0:d0 + T], FA.t[:, c * T:(c + 1) * T], FB.t[:, mu:mu + T], ALU.subtract,
                       [FA.rg(c * T, (c + 1) * T), FB.rg(mu, mu + T)], [FB.rg(d0, d0 + T)])
                    tt("dve", FB.t[:, d0:d0 + T], FB.t[:, d0:d0 + T], FB.t[:, rs:rs + T], ALU.mult,
                       [FB.rg(d0, d0 + T), FB.rg(rs, rs + T)], [FB.rg(d0, d0 + T)])
                    act(H.t[:, c * T:(c + 1) * T], FB.t[:, d0:d0 + T], AF.Silu, [FB.rg(d0, d0 + T), VEC.rg()], [H.rg(c * T, (c + 1) * T)],
                        bias=VEC.t[:, VC_LNB + l * 8 + c:VC_LNB + l * 8 + c + 1], scale=VEC.t[:, VC_LNG + l * 8 + c:VC_LNG + l * 8 + c + 1])
                for oc in range(8):
                    pb = next_ps()
                    for k in range(8):
                        o = 16384 + k * 1024 + oc * 128
                        mm(pb, 0, T, WA.t[:, o:o + 128], WA.rg(o, o + 128), H.t[:, k * T:(k + 1) * T], H.rg(k * T, (k + 1) * T), k == 0, k == 7)
                    t0 = fb_slot()
                    while t0 in (mu, rs):
                        t0 = fb_slot()
                    act(FB.t[:, t0:t0 + T], pb.t[:], AF.Identity, [pb.rg(), SM.rg()], [FB.rg(t0, t0 + T)],
                        bias=SM.t[:, 24 + oc:25 + oc], scale=SM.t[:, 16 + oc:17 + oc])
                    tt("pool", XT.t[:, oc * T:(oc + 1) * T], XT.t[:, oc * T:(oc + 1) * T], FB.t[:, t0:t0 + T], ALU.add,
                       [XT.rg(oc * T, (oc + 1) * T), FB.rg(t0, t0 + T)], [XT.rg(oc * T, (oc + 1) * T)])
                for c in range(8):
                    store_x_chunk(g, c)
                    if g + 1 < NG:
                        load_x_chunk(g + 1, c)

        def mlp_layer(i):
            prep_mod(i, 1)
            load_w(WA, 0, w_m1[i * D:(i + 1) * D, :], 8, 4096, "wa")
            load_w(WB, 0, w_m2[i * 4 * D:(i + 1) * 4 * D, :], 32, 1024, "wb")
            for g in range(NG):
                if g == 0:
                    load_x(g)
                norm_mod(XT, 0, T, H, 0, BFA, 0)

                def w1q(q):
                    hb = (q % 2) * 4096
                    for j in range(8):
                        hc = q * 8 + j
                        pb = next_ps()
                        for k in range(8):
                            o = k * 4096 + hc * 128
                            mm(pb, 0, T, WA.t[:, o:o + 128], WA.rg(o, o + 128), H.t[:, k * T:(k + 1) * T], H.rg(k * T, (k + 1) * T), k == 0, k == 7)
                        r0 = fb_slot()
                        act(FB.t[:, r0:r0 + T], pb.t[:], AF.Relu, [pb.rg()], [FB.rg(r0, r0 + T)])
                        tt("dve" if j % 2 else "pool", BFA.t[:, hb + j * T:hb + (j + 1) * T], FB.t[:, r0:r0 + T], FB.t[:, r0:r0 + T], ALU.mult,
                           [FB.rg(r0, r0 + T)], [BFA.rg(hb + j * T, hb + (j + 1) * T)])

                def w2q(q):
                    hb = (q % 2) * 4096
                    for oc in range(8):
                        pb = next_ps()
                        for j in range(8):
                            o = (q * 8 + j) * 1024 + oc * 128
                            mm(pb, 0, T, WB.t[:, o:o + 128], WB.rg(o, o + 128), BFA.t[:, hb + j * T:hb + (j + 1) * T],
                               BFA.rg(hb + j * T, hb + (j + 1) * T), j == 0, j == 7)
                        stt(XT.t[:, oc * T:(oc + 1) * T], pb.t[:], SM.t[:, 16 + oc:17 + oc], XT.t[:, oc * T:(oc + 1) * T], ALU.mult, ALU.add,
                            [pb.rg(), SM.rg(), XT.rg(oc * T, (oc + 1) * T)], [XT.rg(oc * T, (oc + 1) * T)])
                w1q(0)
                w1q(1)
                w2q(0)
                w1q(2)
                w2q(1)
                w1q(3)
                w2q(2)
                w2q(3)
                for c in range(8):
                    store_x_chunk(g, c)
                    if g + 1 < NG:
                        load_x_chunk(g + 1, c)

        def rope_bank(pbank, tab_ofs, out_ofs):
            t1, t2 = out_ofs, fb_slot()
            while t2 in (tab_ofs, tab_ofs + 512, out_ofs):
                t2 = fb_slot()
            tt("dve", FB.t[:, t1:t1 + 512], pbank.t[:], FB.t[:, tab_ofs:tab_ofs + 512], ALU.mult,
               [pbank.rg(), FB.rg(tab_ofs, tab_ofs + 512)], [FB.rg(t1, t1 + 512)])
            pv = pbank.t[:].rearrange("p (h two d) -> p h two d", h=2, two=2)
            sv = FB.t[:, tab_ofs + 512:tab_ofs + 1024].rearrange("p (h two d) -> p h two d", h=2, two=2)
            ov = FB.t[:, t2:t2 + 512].rearrange("p (h two d) -> p h two d", h=2, two=2)
            for a_, b_ in ((0, 1), (1, 0)):
                tt("dve", ov[:, :, a_, :], pv[:, :, b_, :], sv[:, :, a_, :], ALU.mult,
                   [pbank.rg(), FB.rg(tab_ofs + 512, tab_ofs + 1024)], [FB.rg(t2, t2 + 512)])
            tt("dve", FB.t[:, t1:t1 + 512], FB.t[:, t1:t1 + 512], FB.t[:, t2:t2 + 512], ALU.add,
               [FB.rg(t1, t1 + 512), FB.rg(t2, t2 + 512)], [FB.rg(t1, t1 + 512)])

        def ret_layer(i):
            l = i // 2
            prep_mod(i, 0)
            wbase = w_rin[l * D:(l + 1) * D, :]
            load_w(WA, 0, wbase[:, 1024:2048], 8, 1024, "wa")
            load_w(WA, 8192, wbase[:, 2048:4096], 8, 2048, "wa")
            p.add("pool", lambda e: e.memset(FA.t[:], 0.0), writes=[FA.rg()])
            KZ, VV = 0, 1024
            for g in range(NG):
                if g == 0:
                    load_x(g)
                norm_mod(XT, 0, T, H, 0, BFA, 3072)
                if g + 1 < NG:
                    load_x(g + 1)
                dma("pool", hs_v[:, :, g * T:(g + 1) * T], H.t[:].rearrange("p (k t) -> p k t", k=8), [H.rg()],
                    [HSB.rg(g * BLK, (g + 1) * BLK)], dkey("ho"))
                for t4 in range(4):
                    tI = g * 4 + t4
                    tab = 0
                    dma("sp", FB.t[:, 0:1024], tabs[tI * 128:(tI + 1) * 128, :], [TABB.rg(tI * BLK, (tI + 1) * BLK)], [FB.rg(0, 1024)], dkey("tl"))
                    st["fb"] = 2
                    for j in range(2):
                        pb = next_ps()
                        for k in range(8):
                            o = k * 1024 + j * 512
                            mm(pb, 0, 512, H.t[:, k * T + t4 * 128:k * T + (t4 + 1) * 128], H.rg(k * T + t4 * 128, k * T + (t4 + 1) * 128),
                               WA.t[:, o:o + 512], WA.rg(o, o + 512), k == 0, k == 7)
                        ko = 1024 + j * 512
                        st["fb"] = 4
                        rope_bank(pb, 0, ko)
                        for hh in range(2):
                            hd = j * 2 + hh
                            ts("dve", BFA.t[:, KZ + hd * 256:KZ + (hd + 1) * 256], FB.t[:, ko + hh * 256:ko + (hh + 1) * 256],
                               CST.t[:, 2 + hd:3 + hd], ALU.mult, [FB.rg(ko + hh * 256, ko + (hh + 1) * 256), CST.rg()],
                               [BFA.rg(KZ + hd * 256, KZ + (hd + 1) * 256)])
                    for hd in range(4):
                        pb = next_ps()
                        for k in range(8):
                            o = 8192 + k * 2048 + hd * 512
                            mm(pb, 0, 512, H.t[:, k * T + t4 * 128:k * T + (t4 + 1) * 128], H.rg(k * T + t4 * 128, k * T + (t4 + 1) * 128),
                               WA.t[:, o:o + 512], WA.rg(o, o + 512), k == 0, k == 7)
                        cp("act", BFA.t[:, VV + hd * 512:VV + (hd + 1) * 512], pb.t[:], [pb.rg()], [BFA.rg(VV + hd * 512, VV + (hd + 1) * 512)])
                    dma("pool", kv_t.ap()[tI * 128:(tI + 1) * 128, :], BFA.t[:, 0:3072], [BFA.rg(0, 3072)],
                        [KVB.rg(tI * BLK, (tI + 1) * BLK)], dkey("kv"))
                    for hd in range(4):
                        for hf in range(2):
                            pb = next_ps()
                            ko = KZ + hd * 256 + hf * 128
                            mm(pb, 0, 512, BFA.t[:, ko:ko + 128], BFA.rg(ko, ko + 128), BFA.t[:, VV + hd * 512:VV + (hd + 1) * 512],
                               BFA.rg(VV + hd * 512, VV + (hd + 1) * 512), True, True)
                            so = (hd * 2 + hf) * 512
                            stt(FA.t[:, so:so + 512], FA.t[:, so:so + 512], DEC[hd], pb.t[:], ALU.mult, ALU.add,
                                [FA.rg(so, so + 512), pb.rg()], [FA.rg(so, so + 512)])
            dma("pool", sb_t[l].ap().rearrange("(k p) f -> p k f", p=128), FA.t[:].rearrange("p (k f) -> p k f", k=8),
                [FA.rg()], [SBB[l].rg()], "sx")
            p.add("pool", lambda e: e.collective_compute("AllGather", ALU.bypass, replica_groups=PAIRS,
                                                         ins=[sb_t[l].ap().opt()], outs=[sg_t[l].ap().opt()]),
                  reads=[SBB[l].rg()], writes=[SGB[l].rg()], async_key="cc%d" % i, inc=1)

            SLOT = [(WA, 0), (WA, 16384), (WB, 0), (WB, 16384)]

            def load_head_w(hd):
                wb, wo = SLOT[hd]
                load_w_strided(wb, wo, 1536, wbase[:, hd * 256:(hd + 1) * 256], 8, 256, "wh%d" % hd)
                load_w_strided(wb, wo + 256, 1536, wbase[:, 1024 + hd * 256:1024 + (hd + 1) * 256], 8, 256, "wh%d" % hd)
                load_w_strided(wb, wo + 512, 1536, wbase[:, 2048 + hd * 512:2048 + (hd + 1) * 512], 8, 512, "wh%d" % hd)
                load_w_strided(wb, wo + 1024, 1536, wbase[:, 4096 + hd * 512:4096 + (hd + 1) * 512], 8, 512, "wh%d" % hd)
                load_w(wb, wo + 12288, w_rout[l * 2 * D + hd * 512:l * 2 * D + (hd + 1) * 512, :], 4, 1024, "wh%d" % hd)

            load_head_w(0)
            load_head_w(1)
            STB, QKB, QKT, QXT, KZH, SCT, VH, Y2, Y2T = 0, 1024, 1536, 2048, 2304, 2560, 2688, 3200, 3712
            for hd in range(HEADS):
                wb, wo = SLOT[hd]
                if hd + 2 < HEADS:
                    load_head_w(hd + 2)
                dma("sp", FA.t[:, 0:1024].rearrange("p (k f) -> p k f", k=2),
                    sg_t[l].ap()[hd * 256:(hd + 1) * 256, :].rearrange("(k p) f -> p k f", p=128),
                    [SGB[l].rg()], [FA.rg(0, 1024)], dkey("sl"))
                ts("dve", FA.t[:, 0:1024], FA.t[:, 0:1024], FLAG_AP, ALU.mult, [FA.rg(0, 1024), CST.rg()], [FA.rg(0, 1024)])
                cp("act", BFA.t[:, STB:STB + 1024], FA.t[:, 0:1024], [FA.rg(0, 1024)], [BFA.rg(STB, STB + 1024)])
                gofs = ((0 * 2 + l) * 4 + hd) * 128
                bofs = ((1 * 2 + l) * 4 + hd) * 128
                dma("sp", FA.t[:, 1024:1536], gn_in[gofs:gofs + 128, :], [], [FA.rg(1024, 1536)], dkey("gl"))
                dma("sp", FA.t[:, 1536:2048], gn_in[bofs:bofs + 128, :], [], [FA.rg(1536, 2048)], dkey("gl"))
                for g in range(NG):
                    dma("sp", H.t[:].rearrange("p (k t) -> p k t", k=8), hs_v[:, :, g * T:(g + 1) * T],
                        [HSB.rg(g * BLK, (g + 1) * BLK)], [H.rg()], dkey("hl"))
                    for t4 in range(4):
                        tI = g * 4 + t4
                        dma("sp", FB.t[:, 0:1024], tabs[tI * 128:(tI + 1) * 128, :], [TABB.rg(tI * BLK, (tI + 1) * BLK)], [FB.rg(0, 1024)], dkey("tl"))
                        pqk, pg = next_ps(), next_ps()
                        dma("sp", BFA.t[:, VH:VH + 512], kv_t.ap()[tI * 128:(tI + 1) * 128, 1024 + hd * 512:1024 + (hd + 1) * 512],
                            [KVB.rg(tI * BLK, (tI + 1) * BLK)], [BFA.rg(VH, VH + 512)], dkey("kl"))
                        dma("sp", BFA.t[:, KZH:KZH + 256], kv_t.ap()[tI * 128:(tI + 1) * 128, hd * 256:(hd + 1) * 256],
                            [KVB.rg(tI * BLK, (tI + 1) * BLK)], [BFA.rg(KZH, KZH + 256)], dkey("kl"))
                        for (pb, co) in ((pqk, 0), (pg, 1024)):
                            for k in range(8):
                                o = wo + k * 1536 + co
                                mm(pb, 0, 512, H.t[:, k * T + t4 * 128:k * T + (t4 + 1) * 128], H.rg(k * T + t4 * 128, k * T + (t4 + 1) * 128),
                                   wb.t[:, o:o + 512], wb.rg(o, o + 512), k == 0, k == 7)
                        st["fb"] = 5
                        rope_bank(pqk, 0, 1024)
                        cp("act", BFA.t[:, QKB:QKB + 512], FB.t[:, 1024:1536], [FB.rg(1024, 1536)], [BFA.rg(QKB, QKB + 512)])
                        sgo = 1536
                        act(FB.t[:, sgo:sgo + 512], pg.t[:], AF.Silu, [pg.rg()], [FB.rg(sgo, sgo + 512)])
                        ptr = PB[0]
                        for b4 in range(4):
                            o_ = ptr.t[:, b4 * 128:(b4 + 1) * 128]
                            i_ = BFA.t[:, QKB + b4 * 128:QKB + (b4 + 1) * 128]
                            p.add("pe", (lambda o_, i_: lambda e: e.transpose(o_, i_, IDB.t[:]))(o_, i_),
                                  reads=[BFA.rg(QKB + b4 * 128, QKB + (b4 + 1) * 128), IDB.rg()], writes=[ptr.rg(b4 * 128, (b4 + 1) * 128)])
                        cp("act", BFA.t[:, QKT:QKT + 512], ptr.t[:, 0:512], [ptr.rg(0, 512)], [BFA.rg(QKT, QKT + 512)])
                        tt("dve", BFA.t[:, QXT:QXT + 256], ptr.t[:, 0:256], XI.t[:, hd * 256:(hd + 1) * 256], ALU.mult,
                           [ptr.rg(0, 256), XI.rg()], [BFA.rg(QXT, QXT + 256)])
                        psc = next_ps()
                        for hf in range(2):
                            mm(psc, 0, 128, BFA.t[:, QKT + 256 + hf * 128:QKT + 384 + hf * 128], BFA.rg(QKT + 256 + hf * 128, QKT + 384 + hf * 128),
                               BFA.t[:, QKT + hf * 128:QKT + (hf + 1) * 128], BFA.rg(QKT + hf * 128, QKT + (hf + 1) * 128), hf == 0, hf == 1)
                        tt("dve", BFA.t[:, SCT:SCT + 128], psc.t[:, 0:128], MASK.t[:, hd * 128:(hd + 1) * 128], ALU.mult,
                           [psc.rg(0, 128), MASK.rg()], [BFA.rg(SCT, SCT + 128)])
                        py = next_ps()
                        mm(py, 0, 512, BFA.t[:, SCT:SCT + 128], BFA.rg(SCT, SCT + 128), BFA.t[:, VH:VH + 512], BFA.rg(VH, VH + 512), True, False)
                        for hf in range(2):
                            mm(py, 0, 512, BFA.t[:, QXT + hf * 128:QXT + (hf + 1) * 128], BFA.rg(QXT + hf * 128, QXT + (hf + 1) * 128),
                               BFA.t[:, STB + hf * 512:STB + (hf + 1) * 512], BFA.rg(STB + hf * 512, STB + (hf + 1) * 512), False, hf == 1)
                        for hf in range(2):
                            pst = next_ps()
                            mm(pst, 0, 512, BFA.t[:, KZH + hf * 128:KZH + (hf + 1) * 128], BFA.rg(KZH + hf * 128, KZH + (hf + 1) * 128),
                               BFA.t[:, VH:VH + 512], BFA.rg(VH, VH + 512), True, True)
                            stt(FA.t[:, hf * 512:(hf + 1) * 512], FA.t[:, hf * 512:(hf + 1) * 512], DEC[hd], pst.t[:], ALU.mult, ALU.add,
                                [FA.rg(hf * 512, (hf + 1) * 512), pst.rg()], [FA.rg(hf * 512, (hf + 1) * 512)])
                            cp("act", BFA.t[:, STB + hf * 512:STB + (hf + 1) * 512], FA.t[:, hf * 512:(hf + 1) * 512],
                               [FA.rg(hf * 512, (hf + 1) * 512)], [BFA.rg(STB + hf * 512, STB + (hf + 1) * 512)])
                        bnstats(py)
                        act(SM.t[:, 42:43], SM.t[:, 41:42], AF.Sqrt, [SM.rg(40, 42), CST.rg()], [SM.rg(42, 43)], bias=EPS_AP, scale=1.0)
                        recip(SM, 42, 1)
                        stt(SM.t[:, 43:44], SM.t[:, 40:41], -1.0, SM.t[:, 42:43], ALU.mult, ALU.mult, [SM.rg(40, 43)], [SM.rg(43, 44)])
                        yno = 2048
                        act(FB.t[:, yno:yno + 512], py.t[:], AF.Identity, [py.rg(), SM.rg(42, 44)], [FB.rg(yno, yno + 512)],
                            bias=SM.t[:, 43:44], scale=SM.t[:, 42:43])
                        tt("dve", FB.t[:, yno:yno + 512], FB.t[:, yno:yno + 512], FA.t[:, 1024:1536], ALU.mult,
                           [FB.rg(yno, yno + 512), FA.rg(1024, 1536)], [FB.rg(yno, yno + 512)])
                        tt("dve", FB.t[:, yno:yno + 512], FB.t[:, yno:yno + 512], FA.t[:, 1536:2048], ALU.add,
                           [FB.rg(yno, yno + 512), FA.rg(1536, 2048)], [FB.rg(yno, yno + 512)])
                        tt("dve", BFA.t[:, Y2:Y2 + 512], FB.t[:, yno:yno + 512], FB.t[:, sgo:sgo + 512], ALU.mult,
                           [FB.rg(yno, yno + 512), FB.rg(sgo, sgo + 512)], [BFA.rg(Y2, Y2 + 512)])
                        ptr2 = PB[1]
                        for b4 in range(4):
                            o_ = ptr2.t[:, b4 * 128:(b4 + 1) * 128]
                            i_ = BFA.t[:, Y2 + b4 * 128:Y2 + (b4 + 1) * 128]
                            p.add("pe", (lambda o_, i_: lambda e: e.transpose(o_, i_, IDB.t[:]))(o_, i_),
                                  reads=[BFA.rg(Y2 + b4 * 128, Y2 + (b4 + 1) * 128), IDB.rg()], writes=[ptr2.rg(b4 * 128, (b4 + 1) * 128)])
                        yv = BFA.t[:, Y2T:Y2T + 2048].rearrange("p (k t) -> p k t", k=4)[:, :, t4 * 128:(t4 + 1) * 128]
                        cp("act", yv, ptr2.t[:, 0:512].rearrange("p (k t) -> p k t", k=4), [ptr2.rg(0, 512)], [BFA.rg(Y2T, Y2T + 2048)])
                    load_x(g)
                    for oc in range(8):
                        pb = next_ps()
                        for kk in range(4):
                            o = wo + 12288 + kk * 1024 + oc * 128
                            mm(pb, 0, T, wb.t[:, o:o + 128], wb.rg(o, o + 128), BFA.t[:, Y2T + kk * 512:Y2T + (kk + 1) * 512],
                               BFA.rg(Y2T + kk * 512, Y2T + (kk + 1) * 512), kk == 0, kk == 3)
                        stt(XT.t[:, oc * T:(oc + 1) * T], pb.t[:], SM.t[:, 16 + oc:17 + oc], XT.t[:, oc * T:(oc + 1) * T], ALU.mult, ALU.add,
                            [pb.rg(), SM.rg(16, 24), XT.rg(oc * T, (oc + 1) * T)], [XT.rg(oc * T, (oc + 1) * T)])
                    store_x(g, "pool")

        def final_pass():
            cp("dve", SM.t[:, 0:8], VEC.t[:, VC_FIN:VC_FIN + 8], [VEC.rg()], [SM.rg(0, 8)])
            for g in range(NG):
                if g == 0:
                    load_x(g)
                pb = next_ps()
                for c in range(8):
                    so = (c % 2) * T
                    act(BFA.t[:, so:so + T], XT.t[:, c * T:(c + 1) * T], AF.Square, [XT.rg(c * T, (c + 1) * T)], [BFA.rg(so, so + T)])
                    mm(pb, 0, T, ONES.t[:], ONES.rg(), BFA.t[:, so:so + T], BFA.rg(so, so + T), c == 0, c == 7)
                r0 = 2560
                act(FB.t[:, r0:r0 + T], pb.t[:], AF.Sqrt, [pb.rg(), CST.rg()], [FB.rg(r0, r0 + T)], bias=EPS_AP, scale=1.0)
                recip(FB, r0, T)
                for c in range(8):
                    stt(FA.t[:, c * T:(c + 1) * T], XT.t[:, c * T:(c + 1) * T], SM.t[:, c:c + 1], FB.t[:, r0:r0 + T], ALU.mult, ALU.mult,
                        [XT.rg(c * T, (c + 1) * T), SM.rg(0, 8), FB.rg(r0, r0 + T)], [FA.rg(c * T, (c + 1) * T)])
                if g + 1 < NG:
                    load_x(g + 1)
                for t4 in range(4):
                    oo = (t4 % 2) * 1024
                    for half in range(2):
                        pt = next_ps()
                        for c4 in range(4):
                            c = half * 4 + c4
                            o_ = pt.t[:, c4 * 128:(c4 + 1) * 128]
                            i_ = FA.t[:, c * T + t4 * 128:c * T + (t4 + 1) * 128]
                            p.add("pe", (lambda o_, i_: lambda e: e.transpose(o_, i_, IDF.t[:]))(o_, i_),
                                  reads=[FA.rg(c * T + t4 * 128, c * T + (t4 + 1) * 128), IDF.rg()], writes=[pt.rg(c4 * 128, (c4 + 1) * 128)])
                        cp("act" if half else "dve", FB.t[:, oo + half * 512:oo + (half + 1) * 512], pt.t[:], [pt.rg()],
                           [FB.rg(oo + half * 512, oo + (half + 1) * 512)])
                    r = g * T + t4 * 128
                    dma("sp", out[r:r + 128, :], FB.t[:, oo:oo + 1024], [FB.rg(oo, oo + 1024)], [OUTB.rg((g * 4 + t4) * BLK, (g * 4 + t4 + 1) * BLK)], dkey("oo"))

        stages = []
        for i in range(DEPTH):
            stages.append(("mix", i))
            stages.append(("mlp", i))
        n = len(stages) if stop_after is None else stop_after
        DBGB = Buf("dbg", None, 16 * BLK)
        for si, (kind, i) in enumerate(stages[:n]):
            if kind == "mix":
                if i % 2 == 0:
                    conv_layer(i)
                else:
                    ret_layer(i)
            else:
                mlp_layer(i)
            if debug_dump:
                dma("sp", dbg[si * D:(si + 1) * D, 0:512], xs[:, 0:512], [XSB.rg(0, BLK)], [DBGB.rg(2 * si * BLK, (2 * si + 1) * BLK)], dkey("dg"))
                dma("sp", dbg[si * D:(si + 1) * D, 512:1024], xs[:, NT - 512:NT], [XSB.rg((NG - 1) * BLK, NG * BLK)],
                    [DBGB.rg((2 * si + 1) * BLK, (2 * si + 2) * BLK)], dkey("dg"))
        if stop_after is None:
            final_pass()
        else:
            for g in range(NG):
                load_x(g)
                for t4 in range(4):
                    oo = (t4 % 2) * 1024
                    for half in range(2):
                        pt = next_ps()
                        for c4 in range(4):
                            c = half * 4 + c4
                            o_ = pt.t[:, c4 * 128:(c4 + 1) * 128]
                            i_ = XT.t[:, c * T + t4 * 128:c * T + (t4 + 1) * 128]
                            p.add("pe", (lambda o_, i_: lambda e: e.transpose(o_, i_, IDF.t[:]))(o_, i_),
                                  reads=[XT.rg(c * T + t4 * 128, c * T + (t4 + 1) * 128), IDF.rg()], writes=[pt.rg(c4 * 128, (c4 + 1) * 128)])
                        cp("act" if half else "dve", FB.t[:, oo + half * 512:oo + (half + 1) * 512], pt.t[:], [pt.rg()],
                           [FB.rg(oo + half * 512, oo + (half + 1) * 512)])
                    r = g * T + t4 * 128
                    dma("sp", out[r:r + 128, :], FB.t[:, oo:oo + 1024], [FB.rg(oo, oo + 1024)], [OUTB.rg((g * 4 + t4) * BLK, (g * 4 + t4 + 1) * BLK)], dkey("oo"))
        p.finalize_and_emit()
    return nc


def fm(v):
    v = np.asarray(v, np.float32)
    return np.ascontiguousarray(v.reshape(-1, 128).T)


def make_inputs(inputs):
    f = lambda k: np.asarray(inputs[k], np.float32)
    x, c = f("x"), f("c")
    cols = [fm(f("norm_mix_g").reshape(-1)), fm(f("norm_mlp_g").reshape(-1)), fm(f("final_norm_g")),
            fm(f("conv_b_pw1").reshape(-1)), fm(f("conv_b_dw").reshape(-1)), fm(f("conv_ln_g").reshape(-1)),
            fm(f("conv_ln_b").reshape(-1)), fm(f("conv_b_pw2").reshape(-1))]
    wdw = f("conv_w_dw")
    wdw = wdw.reshape(2, 31, 8, 128).transpose(3, 0, 2, 1).reshape(128, 2 * 8 * 31)
    vec = np.ascontiguousarray(np.concatenate(cols + [wdw], axis=1).astype(np.float32))
    assert vec.shape == (128, NV)
    gn = np.stack([f("ret_gn_g"), f("ret_gn_b")], 0)
    gnrep = np.ascontiguousarray(np.broadcast_to(gn[:, :, :, None, :], (2, 2, 4, 128, 512)).reshape(-1, 512))
    shared = {
        "vec": vec, "gnrep": gnrep,
        "ada_w": f("ada_w").reshape(DEPTH * D, 6 * D), "ada_b": f("ada_b").reshape(1, -1),
        "conv_w_pw1": f("conv_w_pw1").reshape(2 * D, 2 * D), "conv_w_pw2": f("conv_w_pw2").reshape(2 * D, D),
        "ret_w_in": f("ret_w_in").reshape(2 * D, 6 * D), "ret_w_out": f("ret_w_out").reshape(4 * D, D),
        "mlp_w1": f("mlp_w1").reshape(DEPTH * D, 4 * D), "mlp_w2": f("mlp_w2").reshape(DEPTH * 4 * D, D),
    }
    maps = []
    for r in range(8):
        b, half = r // 2, r % 2
        pos = (half * NT + np.arange(NT, dtype=np.float32)).reshape(32, 128).T
        m = dict(shared)
        m["x_sh"] = np.ascontiguousarray(x[b, half * NT:(half + 1) * NT, :])
        m["cvec"] = fm(c[b])
        m["flag"] = np.full((128, 1), float(half), np.float32)
        m["pos"] = np.ascontiguousarray(pos.astype(np.float32))
        maps.append(m)
    return maps


_NC_CACHE = {}


def kernel(**inputs):
    maps = make_inputs(inputs)
    if "nc" not in _NC_CACHE:
        _NC_CACHE["nc"] = build()
    res = run_bass_kernel_spmd(_NC_CACHE["nc"], maps, core_ids=list(range(8)))
    outp = np.empty((4, 2 * NT, D), np.float32)
    for r in range(8):
        outp[r // 2, (r % 2) * NT:(r % 2 + 1) * NT, :] = res.results[r]["out"]
    return outp
```
